# Optimizing a Trainium2 kernel written in Bass

```python
import math
import jax, jax.numpy as jnp
from jax import lax
import numpy as np

D_MODEL = 1024
BATCH = 8
SEQ = 4096
DEPTH = 1
DEC_BATCH = 128
DEC_SEQ = 8
PAST_LEN = 16384
PAGE_SIZE = 128

D_CONV = 512
CONV_WIDTH = 3
N_HEADS = 8
N_KV_HEADS = 2
HEAD_DIM = 64
GROUP = N_HEADS // N_KV_HEADS
D_ATTN = N_HEADS * HEAD_DIM
D_KV = N_KV_HEADS * HEAD_DIM
WINDOW = 128
ROPE_THETA = 10000.0
PEER_HEADS = 8
N_KEYS = 128
N_EXPERTS = N_KEYS * N_KEYS
PEER_TOPK = 16
D_QUERY = 256
D_HALF = D_QUERY // 2
PEER_BLOCK = 256
LN_EPS = 1e-5
DEEPNORM_ALPHA = (2.0 * DEPTH) ** 0.25
DEEPNORM_BETA = (8.0 * DEPTH) ** -0.25
IN_SPLITS = [D_CONV, D_CONV, D_CONV, D_ATTN, D_KV, D_KV, D_MODEL, D_MODEL]
D_IN = sum(IN_SPLITS)

kernel_name = "hybrid_shortconv_swa_sink_peer_deepnorm_step"


def _layer_norm(x, g, b):
    xf = x.astype(jnp.float32)
    mu = jnp.mean(xf, axis=-1, keepdims=True)
    var = jnp.mean(jnp.square(xf - mu), axis=-1, keepdims=True)
    return ((xf - mu) * lax.rsqrt(var + LN_EPS) * g + b).astype(x.dtype)


def _rotary(x, pos):
    half = HEAD_DIM // 2
    inv_freq = ROPE_THETA ** (-2.0 * jnp.arange(half, dtype=jnp.float32) / HEAD_DIM)
    ang = pos.astype(jnp.float32)[:, None] * inv_freq[None, :]
    cos = jnp.cos(ang)[:, None, :]
    sin = jnp.sin(ang)[:, None, :]
    xf = x.astype(jnp.float32)
    x1, x2 = xf[..., :half], xf[..., half:]
    return jnp.concatenate([x1 * cos - x2 * sin, x2 * cos + x1 * sin], axis=-1).astype(x.dtype)


def _mixer_projections(x, pos, w_in):
    B, T, _ = x.shape
    z = jnp.einsum('btd,de->bte', x, w_in)
    cuts = np.cumsum(IN_SPLITS)[:-1].tolist()
    b_gate, c_gate, h_in, q, k, v, g_conv, g_attn = jnp.split(z, cuts, axis=-1)
    q = _rotary(q.reshape(B, T, N_HEADS, HEAD_DIM), pos).reshape(B, T, N_KV_HEADS, GROUP, HEAD_DIM)
    k = _rotary(k.reshape(B, T, N_KV_HEADS, HEAD_DIM), pos)
    v = v.reshape(B, T, N_KV_HEADS, HEAD_DIM)
    return b_gate, c_gate * h_in, q, k, v, g_conv, g_attn


def _short_conv(u, past, conv_w):
    T = u.shape[1]
    up = jnp.concatenate([past, u], axis=1)
    y = conv_w[CONV_WIDTH - 1] * up[:, CONV_WIDTH - 1:CONV_WIDTH - 1 + T]
    for j in range(CONV_WIDTH - 1):
        y = y + conv_w[j] * up[:, j:j + T]
    return y, up[:, -(CONV_WIDTH - 1):]


def _sink_attention(q, k, v, q_pos, k_pos, sinks):
    s = jnp.einsum('...qkgd,...skd->...kgqs', q, k,
                   preferred_element_type=jnp.float32) * (HEAD_DIM ** -0.5)
    d = q_pos[..., :, None] - k_pos[..., None, :]
    mask = (d >= 0) & (d < WINDOW) & (k_pos[..., None, :] >= 0)
    s = jnp.where(mask[..., None, None, :, :], s, -jnp.inf)
    sink = sinks.astype(jnp.float32).reshape(N_KV_HEADS, GROUP)[:, :, None, None]
    m = jnp.maximum(jnp.max(s, axis=-1, keepdims=True), sink)
    p = jnp.exp(s - m)
    denom = jnp.sum(p, axis=-1, keepdims=True) + jnp.exp(sink - m)
    w = (p / denom).astype(v.dtype)
    return jnp.einsum('...kgqs,...skd->...qkgd', w, v)


def _banded_attention(q, k, v, sinks):
    B, S = q.shape[:2]
    nb = S // WINDOW
    qb = q.reshape(B, nb, WINDOW, N_KV_HEADS, GROUP, HEAD_DIM)

    def band(t):
        prev = jnp.pad(t, ((0, 0), (WINDOW, 0), (0, 0), (0, 0)))[:, :S]
        prev = prev.reshape(B, nb, WINDOW, N_KV_HEADS, HEAD_DIM)
        return jnp.concatenate([prev, t.reshape(B, nb, WINDOW, N_KV_HEADS, HEAD_DIM)], axis=2)

    q_pos = jnp.arange(S, dtype=jnp.int32).reshape(nb, WINDOW)
    k_pos = jnp.concatenate([q_pos - WINDOW, q_pos], axis=-1)
    o = _sink_attention(qb, band(k), band(v), q_pos, k_pos, sinks)
    return o.reshape(B, S, D_ATTN)


def _peer(h, w_pq, sub_keys, expert_u, expert_v):
    B, T, D = h.shape
    n = B * T
    flat = jnp.pad(h.reshape(n, D), ((0, (-n) % PEER_BLOCK), (0, 0)))
    blocks = flat.reshape(-1, PEER_BLOCK, D)

    def block_fn(xb):
        qv = jnp.einsum('nd,de->ne', xb, w_pq).reshape(PEER_BLOCK, PEER_HEADS, 2, D_HALF)
        s = jnp.einsum('nhpc,hpkc->nhpk', qv, sub_keys,
                       preferred_element_type=jnp.float32)
        top_s, top_i = lax.top_k(s, PEER_TOPK)
        cand_s = top_s[:, :, 0, :, None] + top_s[:, :, 1, None, :]
        cand_i = top_i[:, :, 0, :, None] * N_KEYS + top_i[:, :, 1, None, :]
        best_s, sel = lax.top_k(cand_s.reshape(PEER_BLOCK, PEER_HEADS, -1), PEER_TOPK)
        idx = jnp.take_along_axis(cand_i.reshape(PEER_BLOCK, PEER_HEADS, -1), sel, axis=-1)
        gate = jax.nn.softmax(best_s, axis=-1)
        pre = jnp.einsum('nd,nhkd->nhk', xb, expert_u[idx], preferred_element_type=jnp.float32)
        coef = (gate * jax.nn.gelu(pre, approximate=False)).astype(xb.dtype)
        return jnp.einsum('nhk,nhkd->nd', coef, expert_v[idx])

    out = lax.map(block_fn, blocks)
    return out.reshape(-1, D)[:n].reshape(B, T, D)


def _merge_and_channel_mix(x, b_gate, conv_y, attn_y, g_conv, g_attn,
                           w_conv_out, w_attn_out, w_o, ln1_g, ln1_b,
                           w_pq, sub_keys, expert_u, expert_v, ln2_g, ln2_b):
    conv_br = jnp.einsum('btc,cd->btd', b_gate * conv_y, w_conv_out)
    attn_br = jnp.einsum('bte,ed->btd', attn_y, w_attn_out)
    merged = jax.nn.sigmoid(g_conv) * conv_br + jax.nn.sigmoid(g_attn) * attn_br
    mix = jnp.einsum('btd,de->bte', merged, w_o)
    h = _layer_norm(DEEPNORM_ALPHA * x + mix, ln1_g, ln1_b)
    return _layer_norm(DEEPNORM_ALPHA * h + _peer(h, w_pq, sub_keys, expert_u, expert_v), ln2_g, ln2_b)


def setup_inputs(seed: int = 0) -> dict:
    key = jax.random.key(seed)
    ks = jax.random.split(key, 20)
    f32 = jnp.float32
    w_buf = min(WINDOW, PAST_LEN)
    nrm = lambda k, shape: jax.random.normal(k, shape, f32)
    col_scale = jnp.concatenate([
        jnp.ones((3 * D_CONV + D_ATTN + D_KV,), f32),
        jnp.full((D_KV,), DEEPNORM_BETA, f32),
        jnp.ones((2 * D_MODEL,), f32)])
    return {
        "x_prompt": nrm(ks[0], (BATCH, SEQ, D_MODEL)),
        "x_sample": nrm(ks[1], (DEC_BATCH, DEC_SEQ, D_MODEL)),
        "state_conv": nrm(ks[2], (DEC_BATCH, CONV_WIDTH - 1, D_CONV)),
        "cache_k": nrm(ks[3], (DEC_BATCH, w_buf, N_KV_HEADS, HEAD_DIM)),
        "cache_v": nrm(ks[4], (DEC_BATCH, w_buf, N_KV_HEADS, HEAD_DIM)) * DEEPNORM_BETA,
        "w_in": nrm(ks[5], (D_MODEL, D_IN)) * D_MODEL ** -0.5 * col_scale,
        "conv_w": nrm(ks[6], (CONV_WIDTH, D_CONV)) * CONV_WIDTH ** -0.5,
        "attn_sinks": nrm(ks[7], (N_HEADS,)) * 0.5,
        "w_conv_out": nrm(ks[8], (D_CONV, D_MODEL)) * D_CONV ** -0.5 * DEEPNORM_BETA,
        "w_attn_out": nrm(ks[9], (D_ATTN, D_MODEL)) * D_ATTN ** -0.5 * DEEPNORM_BETA,
        "w_o": nrm(ks[10], (D_MODEL, D_MODEL)) * D_MODEL ** -0.5 * DEEPNORM_BETA,
        "ln1_g": 1.0 + 0.02 * nrm(ks[11], (D_MODEL,)),
        "ln1_b": 0.02 * nrm(ks[12], (D_MODEL,)),
        "w_pq": nrm(ks[13], (D_MODEL, PEER_HEADS * D_QUERY)) * D_MODEL ** -0.5,
        "sub_keys": nrm(ks[14], (PEER_HEADS, 2, N_KEYS, D_HALF)) * D_HALF ** -0.5,
        "expert_u": nrm(ks[15], (N_EXPERTS, D_MODEL)) * D_MODEL ** -0.5,
        "expert_v": nrm(ks[16], (N_EXPERTS, D_MODEL)) * DEEPNORM_BETA * PEER_HEADS ** -0.5,
        "ln2_g": 1.0 + 0.02 * nrm(ks[17], (D_MODEL,)),
        "ln2_b": 0.02 * nrm(ks[18], (D_MODEL,)),
    }


def reference(x_prompt, x_sample, state_conv, cache_k, cache_v, w_in, conv_w, attn_sinks,
              w_conv_out, w_attn_out, w_o, ln1_g, ln1_b, w_pq, sub_keys, expert_u, expert_v,
              ln2_g, ln2_b):
    B, S, _ = x_prompt.shape
    pos_p = jnp.arange(S, dtype=jnp.int32)
    b_p, u_p, q_p, k_p, v_p, gc_p, ga_p = _mixer_projections(x_prompt, pos_p, w_in)
    conv_past_p = jnp.zeros((B, CONV_WIDTH - 1, D_CONV), x_prompt.dtype)
    conv_p, new_conv_p = _short_conv(u_p, conv_past_p, conv_w)
    attn_p = _banded_attention(q_p, k_p, v_p, attn_sinks)
    y_prompt = _merge_and_channel_mix(x_prompt, b_p, conv_p, attn_p, gc_p, ga_p,
                                      w_conv_out, w_attn_out, w_o, ln1_g, ln1_b,
                                      w_pq, sub_keys, expert_u, expert_v, ln2_g, ln2_b)
    w_p = min(WINDOW, S)
    new_k_p = k_p[:, S - w_p:]
    new_v_p = v_p[:, S - w_p:]

    Bd, T, _ = x_sample.shape
    w_s = cache_k.shape[1]
    pos_s = PAST_LEN + jnp.arange(T, dtype=jnp.int32)
    b_s, u_s, q_s, k_s, v_s, gc_s, ga_s = _mixer_projections(x_sample, pos_s, w_in)
    conv_s, new_conv_s = _short_conv(u_s, state_conv, conv_w)
    k_all = jnp.concatenate([cache_k, k_s], axis=1)
    v_all = jnp.concatenate([cache_v, v_s], axis=1)
    k_pos = jnp.concatenate([PAST_LEN - w_s + jnp.arange(w_s, dtype=jnp.int32), pos_s])
    attn_s = _sink_attention(q_s, k_all, v_all, pos_s, k_pos, attn_sinks).reshape(Bd, T, D_ATTN)
    y_sample = _merge_and_channel_mix(x_sample, b_s, conv_s, attn_s, gc_s, ga_s,
                                      w_conv_out, w_attn_out, w_o, ln1_g, ln1_b,
                                      w_pq, sub_keys, expert_u, expert_v, ln2_g, ln2_b)
    new_k_s = k_all[:, -w_s:]
    new_v_s = v_all[:, -w_s:]

    return (y_prompt, y_sample, new_conv_p, new_k_p, new_v_p, new_conv_s, new_k_s, new_v_s)
```

```python
import contextlib
import os
import numpy as np
import ml_dtypes
import concourse.bass as bass
import concourse.mybir as mybir
from concourse.bass_utils import run_bass_kernel_spmd

F32 = mybir.dt.float32
BF16 = mybir.dt.bfloat16
I32 = mybir.dt.int32
U32 = mybir.dt.uint32
AF = mybir.ActivationFunctionType
ALU = mybir.AluOpType
AX = mybir.AxisListType

NCORES = 8
NT_P = 32
NTILE = 33
ALPHA = float((2.0 * 1) ** 0.25)
LN_EPS = 1e-5
NEG = -30000.0
WCOLS = 4992


class Buf:
    def __init__(self, name, t=None, sem=None):
        self.name = name
        self.t = t
        self.sem = sem
        self.dma_total = 0
        self.last_write = None
        self.reads = []

    def __getitem__(self, idx):
        return self.t[idx]


class _Rec:
    def __getattr__(self, name):
        def f(*a, **k):
            return (name, a, k)
        return f


REC = _Rec()


class Eng:
    def __init__(self, name, sem):
        self.name = name
        self.sem = sem
        self.count = 0
        self.known = {}
        self.prog = []


class FW:
    def __init__(self, nc, stack):
        self.nc = nc
        self.stack = stack
        self.cur = stack
        self.E = {}
        self.sem_bufs = []
        for name in ("pe", "act", "dve", "pool", "sp"):
            self.E[name] = Eng(name, self.new_sem("s_" + name))

    def new_sem(self, name):
        return self.stack.enter_context(self.nc.semaphore(name))

    def sbuf(self, name, shape, dtype, dma=False):
        t = self.cur.enter_context(self.nc.sbuf_tensor(name, shape, dtype))
        b = Buf(name, t, self.new_sem("m_" + name) if dma else None)
        if dma:
            self.sem_bufs.append(b)
        return b

    def psum(self, name, shape, dtype):
        t = self.cur.enter_context(self.nc.psum_tensor(name, shape, dtype))
        return Buf(name, t, None)

    def dram(self, name, shape, dtype, kind):
        t = self.nc.dram_tensor(name, shape, dtype, kind=kind)
        b = Buf(name, t.ap(), self.new_sem("m_" + name))
        self.sem_bufs.append(b)
        return b

    def alias(self, name, t, dma=False):
        b = Buf(name, t, self.new_sem("m_" + name) if dma else None)
        if dma:
            self.sem_bufs.append(b)
        return b

    def _need(self, E, waits, tok):
        if tok is None:
            return
        sem, val = tok
        if E.name == "pe" and sem is E.sem:
            return
        key = id(sem)
        if E.known.get(key, 0) >= val:
            return
        if key not in waits or waits[key][1] < val:
            waits[key] = (sem, val)

    def op(self, ename, fn, reads=(), writes=(), inc=True):
        fn = fn(REC)
        E = self.E[ename]
        waits = {}
        for b in reads:
            self._need(E, waits, b.last_write)
        for b in writes:
            self._need(E, waits, b.last_write)
            for t in b.reads:
                self._need(E, waits, t)
        for key, (sem, val) in waits.items():
            E.known[key] = val
        if inc:
            E.count += 1
            tok = (E.sem, E.count)
        else:
            tok = (E.sem, E.count + 1)
        for b in reads:
            b.reads.append(tok)
            if len(b.reads) > 64:
                b.reads = b.reads[-48:] if False else b.reads
        for b in writes:
            b.last_write = tok
            b.reads = []
        E.prog.append((list(waits.values()), fn, (E.sem, 1) if inc else None))

    def dma(self, qname, fn, dst, src, sem_buf=None):
        fn = fn(REC)
        E = self.E[qname]
        sb = sem_buf if sem_buf is not None else dst
        assert sb.sem is not None, sb.name
        waits = {}
        self._need(E, waits, src.last_write)
        lw = dst.last_write
        if lw is not None and lw[0] is not sb.sem:
            self._need(E, waits, lw)
        for t in dst.reads:
            self._need(E, waits, t)
        for key, (sem, val) in waits.items():
            E.known[key] = val
        sb.dma_total += 16
        tok = (sb.sem, sb.dma_total)
        src.reads.append(tok)
        dst.last_write = tok
        dst.reads = []
        E.prog.append((list(waits.values()), fn, (sb.sem, 16)))
        return tok

    def wait_tok(self, ename, toks):
        E = self.E[ename]
        waits = {}
        for t in toks:
            self._need(E, waits, t)
        for key, (sem, val) in waits.items():
            E.known[key] = val
        E.prog.append((list(waits.values()), None, None))

    def barrier(self):
        toks = []
        for E in self.E.values():
            if E.count > 0:
                toks.append((E.sem, E.count))
        for b in self.sem_bufs:
            if b.dma_total > 0:
                toks.append((b.sem, b.dma_total))
        for name in self.E:
            self.wait_tok(name, toks)

    def run_block(self):
        nc = self.nc
        with nc.Block() as block:
            def mk(E):
                def body(eng):
                    for waits, fn, inc in E.prog:
                        for sem, val in waits:
                            eng.wait_ge(sem, val)
                        if fn is None:
                            continue
                        ins = getattr(eng, fn[0])(*fn[1], **fn[2])
                        if inc is not None:
                            ins.then_inc(inc[0], inc[1])
                return body
            block.tensor(mk(self.E["pe"]))
            block.scalar(mk(self.E["act"]))
            block.vector(mk(self.E["dve"]))
            block.gpsimd(mk(self.E["pool"]))
            block.sync(mk(self.E["sp"]))
        for E in self.E.values():
            E.prog = []


class Rot:
    def __init__(self, bufs):
        self.bufs = bufs
        self.i = 0

    def next(self):
        b = self.bufs[self.i % len(self.bufs)]
        self.i += 1
        return b


def bcast(ap, axis, shape):
    return ap.unsqueeze(axis).to_broadcast(shape)


def layer_norm(fw, y, out, g_rep, b_rep, tmp_pool, mul_eng="dve"):
    st6, mv, sd, rs, nmr = (tmp_pool[k] for k in ("st6", "mv", "sd", "rs", "nmr"))
    for hf in range(2):
        fw.op("dve", lambda e, hf=hf: e.bn_stats(out=st6[:, hf, :], in_=y[:, hf * 512:(hf + 1) * 512]),
              reads=[y], writes=[st6])
    fw.op("dve", lambda e: e.bn_aggr(out=mv[:], in_=st6[:].rearrange("p a b -> p (a b)")), reads=[st6], writes=[mv])
    fw.op("act", lambda e: e.activation(out=sd[:], in_=mv[:, 1:2], func=AF.Sqrt, bias=tmp_pool["eps"][:], scale=1.0),
          reads=[mv, tmp_pool["eps"]], writes=[sd])
    fw.op("dve", lambda e: e.reciprocal(out=rs[:], in_=sd[:]), reads=[sd], writes=[rs])
    fw.op("dve", lambda e: e.tensor_scalar(out=nmr[:], in0=mv[:, 0:1], scalar1=rs[:, 0:1], scalar2=-1.0,
                                           op0=ALU.mult, op1=ALU.mult), reads=[mv, rs], writes=[nmr])
    fw.op("act", lambda e: e.activation(out=y[:], in_=y[:], func=AF.Identity, bias=nmr[:, 0:1], scale=rs[:, 0:1]),
          reads=[y, nmr, rs], writes=[y])
    fw.op(mul_eng, lambda e: e.tensor_tensor(out=y[:], in0=y[:], in1=g_rep[:], op=ALU.mult), reads=[y, g_rep], writes=[y])
    fw.op("pool", lambda e: e.tensor_tensor(out=out[:], in0=y[:], in1=b_rep[:], op=ALU.add), reads=[y, b_rep], writes=[out])


def build_program(stages=("pre", "a", "b1", "b2"), debug=False, tile_list=None):
    tile_list = list(range(NTILE)) if tile_list is None else list(tile_list)
    nc = bass.Bass("TRN2", target_bir_lowering=False)
    with contextlib.ExitStack() as st:
        fw = FW(nc, st)
        D = {}

        def din(name, shape, dtype=F32):
            D[name] = fw.dram(name, shape, dtype, "ExternalInput")
            return D[name]

        def dout(name, shape, dtype=F32):
            D[name] = fw.dram(name, shape, dtype, "ExternalOutput")
            return D[name]

        dbg = "ExternalOutput" if debug else "Internal"

        xin = din("xin", [NTILE, 128, 1024])
        sconv = din("sconv", [32, 512])
        ck = din("ck", [16, 128, 128])
        cv = din("cv", [16, 128, 128])
        w_in = din("w_in", [1024, WCOLS])
        cw_in = din("cw", [128, 4, 3])
        sinks = din("sinks", [8])
        w_co = din("w_co", [512, 1024])
        w_ao = din("w_ao", [512, 1024])
        w_o = din("w_o", [1024, 1024])
        ln1g = din("ln1g", [1024]); ln1b = din("ln1b", [1024])
        ln2g = din("ln2g", [1024]); ln2b = din("ln2b", [1024])
        w_pq = din("w_pq", [1024, 2048])
        skeys = din("skeys", [16, 128, 128])
        eu = din("eu", [16384, 1024])
        ev = din("ev", [16384, 1024])
        ropec = din("ropec", [128, NTILE * 128])
        ropes = din("ropes", [128, NTILE * 128])
        c_ident = din("c_ident", [128, 128])
        c_masks = din("c_masks", [3, 128, 512])
        c_mcache = din("c_mcache", [128, 8])
        c_iota = din("c_iota", [128, 128])

        yout = dout("yout", [NTILE, 128, 1024])
        scp = dout("scp", [2, 512])
        ckp = dout("ckp", [128, 128])
        cvp = dout("cvp", [128, 128])
        scs = dout("scs", [16, 2, 512])
        cks = dout("cks", [16, 128, 128])
        cvs = dout("cvs", [16, 128, 128])

        UT = fw.dram("UT", [128, 128, 1024], BF16, "Internal")
        VB = fw.dram("VB", [16384, 1024], BF16, "Internal")
        HS = fw.dram("HS", [NTILE, 128, 1024], F32, dbg)
        HT = fw.dram("HT", [NTILE, 128, 8, 128], BF16, "Internal")
        SEL = fw.dram("SEL", [NTILE, 128, 3, 128], F32, dbg)

        out_toks = []

        pre_alone = ("a" not in stages) or bool(os.environ.get("PRE_STANDALONE"))
        if "pre" in stages and pre_alone:
            with contextlib.ExitStack() as ps:
                fw.cur = ps
                identb = fw.sbuf("p_identb", [128, 128], BF16, dma=True)
                fw.dma("pool", lambda e: e.dma_start(out=identb[:], in_=c_ident[:]), identb, c_ident)
                for i in range(32):
                    fw.dma("pool", lambda e, i=i: e.dma_start(out=VB[i * 512:(i + 1) * 512, :], in_=ev[i * 512:(i + 1) * 512, :]), VB, ev)
                ub = [fw.sbuf(f"p_ub{i}", [128, 1024], BF16, dma=True) for i in range(3)]
                utb = [fw.sbuf(f"p_utb{i}", [128, 1024], BF16, dma=True) for i in range(3)]
                pb = [fw.psum(f"p_ps{i}", [128, 1024], BF16) for i in range(4)]
                for c in range(128):
                    u = ub[c % 3]; ut = utb[c % 3]; p = pb[c % 4]
                    fw.dma("pool", lambda e, c=c, u=u: e.dma_start(out=u[:], in_=eu[c * 128:(c + 1) * 128, :]), u, eu)
                    for k in range(8):
                        fw.op("pe", lambda e, k=k, u=u, p=p: e.transpose(out=p[:, k * 128:(k + 1) * 128], in_=u[:, k * 128:(k + 1) * 128], identity=identb[:]),
                              reads=[u, identb], writes=[p], inc=(k == 7))
                    if c % 2 == 0:
                        fw.op("act", lambda e, ut=ut, p=p: e.activation(out=ut[:], in_=p[:], func=AF.Copy), reads=[p], writes=[ut])
                    else:
                        fw.op("dve", lambda e, ut=ut, p=p: e.tensor_copy(out=ut[:], in_=p[:]), reads=[p], writes=[ut])
                    fw.dma("sp", lambda e, c=c, ut=ut: e.dma_start(out=UT[c], in_=ut[:]), UT, ut, sem_buf=ut)
                fw.barrier()
                fw.run_block()

        if "a" in stages:
            with contextlib.ExitStack() as ps:
                fw.cur = ps
                wi = fw.sbuf("a_wi", [128, 8, WCOLS], BF16, dma=True)
                for k in range(8):
                    fw.dma("pool", lambda e, k=k: e.dma_start(out=wi[:, k, :], in_=w_in[k * 128:(k + 1) * 128, :]), wi, w_in)
                wco = fw.sbuf("a_wco", [128, 4, 1024], BF16, dma=True)
                fw.dma("pool", lambda e: e.dma_start(out=wco[:], in_=w_co.t.rearrange("(k p) c -> p k c", p=128)), wco, w_co)
                wao = fw.sbuf("a_wao", [128, 4, 1024], BF16, dma=True)
                fw.dma("pool", lambda e: e.dma_start(out=wao[:], in_=w_ao.t.rearrange("(k p) c -> p k c", p=128)), wao, w_ao)
                wo = fw.sbuf("a_wo", [128, 8, 1024], BF16, dma=True)
                fw.dma("pool", lambda e: e.dma_start(out=wo[:], in_=w_o.t.rearrange("(k p) c -> p k c", p=128)), wo, w_o)
                cwt = fw.sbuf("a_cw", [128, 4, 3], F32, dma=True)
                fw.dma("sp", lambda e: e.dma_start(out=cwt[:], in_=cw_in[:]), cwt, cw_in)
                es = fw.sbuf("a_es", [128, 4], F32, dma=True)
                fw.dma("sp", lambda e: e.dma_start(out=es[0:64, :], in_=sinks.t[0:4].partition_broadcast(64)), es, sinks)
                fw.dma("sp", lambda e: e.dma_start(out=es[64:128, :], in_=sinks.t[4:8].partition_broadcast(64)), es, sinks)
                fw.op("act", lambda e: e.activation(out=es[:], in_=es[:], func=AF.Exp), reads=[es], writes=[es])
                g1 = fw.sbuf("a_g1", [128, 1024], F32, dma=True)
                b1 = fw.sbuf("a_b1", [128, 1024], F32, dma=True)
                fw.dma("sp", lambda e: e.dma_start(out=g1[:], in_=ln1g.t.partition_broadcast(128)), g1, ln1g)
                fw.dma("sp", lambda e: e.dma_start(out=b1[:], in_=ln1b.t.partition_broadcast(128)), b1, ln1b)
                identf = fw.sbuf("a_identf", [128, 128], F32, dma=True)
                fw.dma("sp", lambda e: e.dma_start(out=identf[:], in_=c_ident[:]), identf, c_ident)
                identb = fw.sbuf("a_identb", [128, 128], BF16, dma=True)
                fw.dma("pool", lambda e: e.dma_start(out=identb[:], in_=c_ident[:]), identb, c_ident)
                masks = fw.sbuf("a_masks", [128, 3, 512], BF16, dma=True)
                fw.dma("pool", lambda e: e.dma_start(out=masks[:], in_=c_masks.t.rearrange("m p f -> p m f")), masks, c_masks)
                mcache = fw.sbuf("a_mcache", [128, 8], BF16, dma=True)
                fw.dma("pool", lambda e: e.dma_start(out=mcache[:], in_=c_mcache[:]), mcache, c_mcache)
                oz = [fw.sbuf(f"a_oz{i}", [128, 128], BF16) for i in range(2)]
                for i in range(2):
                    fw.op("pool", lambda e, i=i: e.memset(oz[i][:], 0.0), writes=[oz[i]])
                    fw.op("pool", lambda e, i=i: e.memset(oz[i][:, i * 64:(i + 1) * 64], 1.0), writes=[oz[i]])
                eps = fw.sbuf("a_eps", [128, 1], F32)
                fw.op("pool", lambda e: e.memset(eps[:], LN_EPS), writes=[eps])

                xt = [fw.sbuf(f"a_xt{i}", [128, 1024], F32, dma=True) for i in range(2)]
                cosb = [fw.sbuf(f"a_cos{i}", [128, 128], F32, dma=True) for i in range(2)]
                sinb = [fw.sbuf(f"a_sin{i}", [128, 128], F32, dma=True) for i in range(2)]
                xTs = [fw.sbuf(f"a_xT{i}", [128, 8, 128], BF16) for i in range(2)]
                bT = fw.sbuf("a_bT", [128, 4, 128], F32)
                cT = fw.sbuf("a_cT", [128, 4, 128], F32)
                ubp = fw.sbuf("a_ubp", [128, 4, 1, 130], F32)
                ubs = fw.sbuf("a_ubs", [128, 4, 16, 10], F32)
                fw.op("pool", lambda e: e.memset(ubp[:], 0.0), writes=[ubp])
                cy = fw.sbuf("a_cy", [128, 4, 128], F32)
                ct1 = fw.sbuf("a_ct1", [128, 4, 128], F32)
                tq1 = fw.sbuf("a_tq1", [128, 4, 128], F32)
                tq2 = fw.sbuf("a_tq2", [128, 4, 128], F32)
                mt1 = tq1; mt2 = tq2
                bcT = fw.sbuf("a_bcT", [128, 4, 128], BF16)
                rqT = fw.sbuf("a_rqT", [128, 4, 128], BF16)
                tk1 = fw.sbuf("a_tk1", [128, 128], F32)
                tk2 = fw.sbuf("a_tk2", [128, 128], F32)
                rkT = [fw.sbuf(f"a_rkT{i}", [128, 128], BF16) for i in range(2)]
                rkf = fw.sbuf("a_rkf", [128, 128], F32)
                vz = [[fw.sbuf(f"a_vz{i}{j}", [128, 128], BF16) for j in range(2)] for i in range(2)]
                for i in range(2):
                    for j in range(2):
                        fw.op("pool", lambda e, i=i, j=j: e.memset(vz[i][j][:], 0.0), writes=[vz[i][j]])
                vf = fw.sbuf("a_vf", [128, 128], F32, dma=True)
                kf = fw.sbuf("a_kf", [128, 128], F32, dma=True)
                uf = fw.sbuf("a_uf", [128, 512], F32, dma=True)
                sgc = fw.sbuf("a_sgc", [128, 8, 128], F32)
                sga = fw.sbuf("a_sga", [128, 8, 128], F32)
                PT = [fw.sbuf(f"a_PT{i}", [128, 512], BF16) for i in range(4)]
                den = fw.sbuf("a_den", [128, 512], F32)
                aoT = fw.sbuf("a_aoT", [128, 4, 128], BF16)
                mT = fw.sbuf("a_mT", [128, 8, 128], BF16)
                y1 = fw.sbuf("a_y1", [128, 1024], F32)
                hh = [fw.sbuf(f"a_h{i}", [128, 1024], F32, dma=True) for i in range(1)]
                lnp = dict(st6=fw.sbuf("a_st6", [128, 2, 6], F32), mv=fw.sbuf("a_mv", [128, 2], F32),
                           sd=fw.sbuf("a_sd", [128, 1], F32), rs=fw.sbuf("a_rs", [128, 1], F32),
                           nmr=fw.sbuf("a_nmr", [128, 1], F32), eps=eps)
                scl = fw.sbuf("a_scl", [32, 512], F32, dma=True)
                ckb = [fw.sbuf(f"a_ckb{i}", [128, 128], BF16, dma=True) for i in range(2)]
                cvz = [fw.sbuf(f"a_cvz{i}", [128, 16, 128], BF16, dma=True) for i in range(2)]
                ckT = fw.sbuf("a_ckT", [128, 16, 128], BF16)
                PC = fw.sbuf("a_PC", [128, 2, 16, 32], BF16)
                numc = fw.sbuf("a_numc", [128, 512], F32)

                banks = Rot([fw.psum(f"a_ps{i}", [128, 512], F32) for i in range(8)])

                class PreWork:
                    def __init__(self):
                        self.ub = [fw.alias(f"a_pub{i}", cvz[i][:].rearrange("p a b -> p (a b)").bitcast(F32), dma=True) for i in range(2)]
                        v2 = lambda b_, h_: b_[:, h_ * 8:(h_ + 1) * 8, :].rearrange("p a b -> p (a b)")
                        self.utb = [fw.alias(f"a_putb{i}", v2(ckT, i), dma=True) for i in range(2)]
                        self.shared = [(self.ub[0], cvz[0]), (self.ub[1], cvz[1]), (self.utb[0], ckT), (self.utb[1], ckT)]
                        self.nl = 0
                        self.nc_ = 0
                        self.nv = 0

                    def handoff(self, to_sample):
                        for ab, tb in self.shared:
                            src_, dst_ = (ab, tb) if to_sample else (tb, ab)
                            dst_.reads = dst_.reads + src_.reads + ([src_.last_write] if src_.last_write is not None else [])

                    def load(self, cnt):
                        for _ in range(cnt):
                            if self.nl >= 128:
                                return
                            c = self.nl; self.nl += 1
                            u = self.ub[c % 2]
                            fw.dma("sp", lambda e: e.dma_start(out=u[:], in_=eu[c * 128:(c + 1) * 128, :]), u, eu)

                    def vconv(self, cnt):
                        for _ in range(cnt):
                            if self.nv >= 32:
                                return
                            i = self.nv; self.nv += 1
                            fw.dma("pool", lambda e: e.dma_start(out=VB[i * 512:(i + 1) * 512, :], in_=ev[i * 512:(i + 1) * 512, :]), VB, ev)

                    def step(self, cnt):
                        for _ in range(cnt):
                            if self.nc_ >= 128:
                                return
                            c = self.nc_; self.nc_ += 1
                            while self.nl <= c:
                                self.load(1)
                            u = self.ub[c % 2]; ut = self.utb[c % 2]
                            pq = [banks.next(), banks.next()]
                            for k in range(8):
                                p = pq[k // 4]
                                fw.op("pe", lambda e, k=k, p=p: e.transpose(out=p[:, (k % 4) * 128:(k % 4 + 1) * 128], in_=u[:, k * 128:(k + 1) * 128], identity=identf[:]),
                                      reads=[u, identf], writes=[p], inc=(k % 4 == 3))
                            self.load(1)
                            for hf_ in range(2):
                                fw.op("act", lambda e, hf_=hf_: e.activation(out=ut[:, hf_ * 512:(hf_ + 1) * 512], in_=pq[hf_][:], func=AF.Copy), reads=[pq[hf_]], writes=[ut])
                            fw.dma("sp", lambda e: e.dma_start(out=UT[c], in_=ut[:]), UT, ut, sem_buf=ut)

                prew = PreWork() if ("pre" in stages and not pre_alone) else None
                if prew is not None:
                    prew.load(2)

                def load_tile(t, s):
                    fw.dma("sp", lambda e: e.dma_start(out=xt[s][:], in_=xin[t]), xt[s], xin)
                    fw.dma("sp", lambda e: e.dma_start(out=cosb[s][:], in_=ropec[:, t * 128:(t + 1) * 128]), cosb[s], ropec)
                    fw.dma("sp", lambda e: e.dma_start(out=sinb[s][:], in_=ropes[:, t * 128:(t + 1) * 128]), sinb[s], ropes)

                load_tile(tile_list[0], 0)

                def emit_xT(ti):
                    s = ti % 2
                    x = xt[s]; xT = xTs[s]
                    for hf in range(2):
                        p = banks.next()
                        for kk in range(4):
                            k = hf * 4 + kk
                            fw.op("pe", lambda e, k=k, kk=kk, p=p: e.transpose(out=p[:, kk * 128:(kk + 1) * 128], in_=x[:, k * 128:(k + 1) * 128], identity=identf[:]),
                                  reads=[x, identf], writes=[p], inc=(kk == 3))
                        if hf == 0:
                            fw.op("act", lambda e, hf=hf, p=p: e.activation(out=xT[:, hf * 4:(hf + 1) * 4, :], in_=p[:].rearrange("p (a b) -> p a b", a=4), func=AF.Copy), reads=[p], writes=[xT])
                        else:
                            fw.op("dve", lambda e, hf=hf, p=p: e.tensor_copy(out=xT[:, hf * 4:(hf + 1) * 4, :], in_=p[:].rearrange("p (a b) -> p a b", a=4)), reads=[p], writes=[xT])

                def proj(xT, c0, nch):
                    p = banks.next()
                    for j in range(nch):
                        for k in range(8):
                            fw.op("pe", lambda e, j=j, k=k, p=p: e.matmul(p[:, j * 128:(j + 1) * 128], lhsT=wi[:, k, (c0 + j) * 128:(c0 + j + 1) * 128], rhs=xT[:, k, :], start=(k == 0), stop=(k == 7)),
                                  reads=[wi, xT], writes=[p], inc=(j == nch - 1 and k == 7))
                    return p

                def v4(p):
                    return p[:].rearrange("p (a b) -> p a b", a=4)

                def emit_bch(ti):
                    t = tile_list[ti]
                    samp = (t == NTILE - 1)
                    xT = xTs[ti % 2]
                    p = proj(xT, 0, 4)
                    fw.op("act", lambda e, p=p: e.activation(out=bT[:], in_=v4(p), func=AF.Copy), reads=[p], writes=[bT])
                    p = proj(xT, 4, 4)
                    fw.op("act", lambda e, p=p: e.activation(out=cT[:], in_=v4(p), func=AF.Copy), reads=[p], writes=[cT])
                    p = proj(xT, 8, 4)
                    ub = ubs if samp else ubp
                    if samp:
                        udst = lambda: ubs[:, :, :, 2:10]
                        uin = lambda ap: ap.rearrange("p a (b t) -> p a b t", t=8)
                    else:
                        udst = lambda: ubp[:, :, 0, 2:130]
                        uin = lambda ap: ap
                    fw.op("dve", lambda e, p=p: e.tensor_tensor(out=udst(), in0=uin(v4(p)), in1=uin(cT[:]), op=ALU.mult), reads=[p, cT], writes=[ub])
                    if samp:
                        fw.dma("sp", lambda e: e.dma_start(out=scl[:], in_=sconv[:]), scl, sconv)
                        p = banks.next()
                        for c in range(4):
                            fw.op("pe", lambda e, c=c, p=p: e.transpose(out=p[:, c * 32:(c + 1) * 32], in_=scl[:, c * 128:(c + 1) * 128], identity=identf[0:32, 0:32]),
                                  reads=[scl, identf], writes=[p], inc=(c == 3))
                        fw.op("act", lambda e, p=p: e.activation(out=ubs[:, :, :, 0:2], in_=p[:, 0:128].rearrange("p (a b j) -> p a b j", a=4, j=2), func=AF.Copy), reads=[p], writes=[ubs])

                for ti, t in enumerate(tile_list):
                    samp = (t == NTILE - 1)
                    s = ti % 2
                    if samp and prew is not None:
                        while prew.nc_ < 128:
                            prew.step(4)
                        prew.handoff(True)
                    if ti + 1 < len(tile_list):
                        load_tile(tile_list[ti + 1], 1 - s)
                    x = xt[s]; cs = cosb[s]; sn = sinb[s]
                    xT = xTs[s]
                    if ti == 0:
                        emit_xT(0)

                    ub = ubs if samp else ubp
                    if ti == 0:
                        emit_bch(0)
                    if prew is not None and not samp:
                        prew.step(1)
                    if samp:
                        usl = lambda o: ubs[:, :, :, o:o + 8]
                        cwb = lambda j: cwt[:, :, j:j + 1].unsqueeze(3).to_broadcast([128, 4, 16, 8])
                        v = lambda ap: ap.rearrange("p a (b t) -> p a b t", t=8)
                    else:
                        usl = lambda o: ubp[:, :, 0, o:o + 128]
                        cwb = lambda j: cwt[:, :, j:j + 1].to_broadcast([128, 4, 128])
                        v = lambda ap: ap
                    fw.op("pool", lambda e: e.tensor_tensor(out=v(cy[:]), in0=usl(2), in1=cwb(2), op=ALU.mult), reads=[ub, cwt], writes=[cy])
                    fw.op("pool", lambda e: e.tensor_tensor(out=v(ct1[:]), in0=usl(1), in1=cwb(1), op=ALU.mult), reads=[ub, cwt], writes=[ct1])
                    fw.op("pool", lambda e: e.tensor_tensor(out=cy[:], in0=cy[:], in1=ct1[:], op=ALU.add), reads=[cy, ct1], writes=[cy])
                    fw.op("pool", lambda e: e.tensor_tensor(out=v(ct1[:]), in0=usl(0), in1=cwb(0), op=ALU.mult), reads=[ub, cwt], writes=[ct1])
                    fw.op("pool", lambda e: e.tensor_tensor(out=cy[:], in0=cy[:], in1=ct1[:], op=ALU.add), reads=[cy, ct1], writes=[cy])
                    fw.op("pool", lambda e: e.tensor_tensor(out=bcT[:], in0=cy[:], in1=bT[:], op=ALU.mult), reads=[cy, bT], writes=[bcT])
                    if samp or t == NT_P - 1:
                        fw.op("pool", lambda e: e.tensor_copy(out=v(ct1[:]), in_=usl(2)), reads=[ub], writes=[ct1])
                        p = banks.next()
                        for c in range(4):
                            fw.op("pe", lambda e, c=c, p=p: e.transpose(out=p[:, c * 128:(c + 1) * 128], in_=ct1[:, c, :], identity=identf[:]),
                                  reads=[ct1, identf], writes=[p], inc=(c == 3))
                        fw.op("act", lambda e, p=p: e.activation(out=uf[:], in_=p[:], func=AF.Copy), reads=[p], writes=[uf])
                        if samp:
                            for b in range(16):
                                out_toks.append(fw.dma("sp", lambda e, b=b: e.dma_start(out=scs[b], in_=uf[b * 8 + 6:b * 8 + 8, :]), scs, uf))
                        elif not os.environ.get("SKIP_SCP"):
                            out_toks.append(fw.dma("sp", lambda e: e.dma_start(out=scp[:], in_=uf[126:128, :]), scp, uf))
                    if not samp:
                        fw.op("pool", lambda e: e.tensor_copy(out=ubp[:, :, 0, 0:2], in_=ubp[:, :, 0, 128:130]), reads=[ubp], writes=[ubp])
                    csb = lambda: cs[:].unsqueeze(1).to_broadcast([128, 4, 128])
                    snb = lambda: sn[:].unsqueeze(1).to_broadcast([128, 4, 128])
                    p = proj(xT, 12, 4)
                    fw.op("dve", lambda e, p=p: e.tensor_tensor(out=tq1[:], in0=v4(p), in1=csb(), op=ALU.mult), reads=[p, cs], writes=[tq1])
                    p = proj(xT, 16, 4)
                    fw.op("dve", lambda e, p=p: e.tensor_tensor(out=tq2[:], in0=v4(p), in1=snb(), op=ALU.mult), reads=[p, sn], writes=[tq2])
                    fw.op("dve", lambda e: e.tensor_tensor(out=rqT[:], in0=tq1[:], in1=tq2[:], op=ALU.add), reads=[tq1, tq2], writes=[rqT])
                    p = proj(xT, 20, 2)
                    rk = rkT[s]
                    fw.op("dve", lambda e, p=p: e.tensor_tensor(out=tk1[:], in0=p[:, 0:128], in1=cs[:], op=ALU.mult), reads=[p, cs], writes=[tk1])
                    fw.op("dve", lambda e, p=p: e.tensor_tensor(out=tk2[:], in0=p[:, 128:256], in1=sn[:], op=ALU.mult), reads=[p, sn], writes=[tk2])
                    fw.op("dve", lambda e: e.tensor_tensor(out=rk[:], in0=tk1[:], in1=tk2[:], op=ALU.add), reads=[tk1, tk2], writes=[rk])
                    last = samp or t == NT_P - 1
                    if last:
                        fw.op("pool", lambda e: e.tensor_tensor(out=rkf[:], in0=tk1[:], in1=tk2[:], op=ALU.add), reads=[tk1, tk2], writes=[rkf])
                        p = banks.next()
                        fw.op("pe", lambda e, p=p: e.transpose(out=p[:, 0:128], in_=rkf[:], identity=identf[:]), reads=[rkf, identf], writes=[p])
                        fw.op("act", lambda e, p=p: e.activation(out=kf[:], in_=p[:, 0:128], func=AF.Copy), reads=[p], writes=[kf])
                        if samp:
                            for b in range(16):
                                out_toks.append(fw.dma("sp", lambda e, b=b: e.dma_start(out=cks[b, 120:128, :], in_=kf[b * 8:(b + 1) * 8, :]), cks, kf))
                        elif not os.environ.get("SKIP_CKP"):
                            out_toks.append(fw.dma("sp", lambda e: e.dma_start(out=ckp[:], in_=kf[:]), ckp, kf))
                    if prew is not None and not samp:
                        prew.step(1)
                    p = banks.next()
                    for k in range(8):
                        fw.op("pe", lambda e, k=k, p=p: e.matmul(p[:, 0:128], lhsT=xT[:, k, :], rhs=wi[:, k, 22 * 128:23 * 128], start=(k == 0), stop=(k == 7)),
                              reads=[xT, wi], writes=[p], inc=(k == 7))
                    for j in range(2):
                        fw.op("act", lambda e, p=p, j=j: e.activation(out=vz[s][j][:, j * 64:(j + 1) * 64], in_=p[:, j * 64:(j + 1) * 64], func=AF.Copy), reads=[p], writes=[vz[s][j]])
                    if last:
                        fw.op("act", lambda e, p=p: e.activation(out=vf[:], in_=p[:, 0:128], func=AF.Copy), reads=[p], writes=[vf])
                        if samp:
                            for b in range(16):
                                out_toks.append(fw.dma("sp", lambda e, b=b: e.dma_start(out=cvs[b, 120:128, :], in_=vf[b * 8:(b + 1) * 8, :]), cvs, vf))
                        elif not os.environ.get("SKIP_CVP"):
                            out_toks.append(fw.dma("sp", lambda e: e.dma_start(out=cvp[:], in_=vf[:]), cvp, vf))
                    if prew is not None and not samp:
                        prew.step(1)
                    blocks = []
                    if samp:
                        blocks.append((rkT[s], vz[s], 2))
                    else:
                        if ti > 0:
                            blocks.append((rkT[1 - s], vz[1 - s], 1))
                        blocks.append((rkT[s], vz[s], 0))
                    pts = {}
                    for kv in range(2):
                        lo = kv * 64
                        for bi, (rkb, vb, mi) in enumerate(blocks):
                            p = banks.next()
                            fw.op("pe", lambda e, p=p, mi=mi: e.matmul(p[:], lhsT=identb[:], rhs=masks[:, mi, :], start=True, stop=False),
                                  reads=[identb, masks], writes=[p], inc=False)
                            fw.op("pe", lambda e, p=p, rkb=rkb, lo=lo: e.matmul(p[:], lhsT=rkb[lo:lo + 64, :], rhs=rqT[lo:lo + 64, :, :].rearrange("p a b -> p (a b)"), start=False, stop=True),
                                  reads=[rkb, rqT], writes=[p])
                            pt = PT[kv * 2 + bi]
                            fw.op("act", lambda e, p=p, pt=pt: e.activation(out=pt[:], in_=p[:], func=AF.Exp, scale=0.125), reads=[p], writes=[pt])
                            pts[(kv, bi)] = pt
                    for g in range(2):
                        p = proj(xT, 23 + g * 4, 4)
                        fw.op("act", lambda e, p=p, g=g: e.activation(out=sgc[:, g * 4:(g + 1) * 4, :], in_=v4(p), func=AF.Sigmoid), reads=[p], writes=[sgc])
                    for g in range(2):
                        p = proj(xT, 31 + g * 4, 4)
                        fw.op("act", lambda e, p=p, g=g: e.activation(out=sga[:, g * 4:(g + 1) * 4, :], in_=v4(p), func=AF.Sigmoid), reads=[p], writes=[sga])
                    if samp:
                        for j in range(2):
                            fw.op("pool", lambda e, j=j: e.memset(cvz[j][:], 0.0), writes=[cvz[j]])
                            fw.dma("pool", lambda e, j=j: e.dma_start(out=cvz[j][:, :, j * 64:(j + 1) * 64], in_=cv.t.rearrange("b k c -> k b c")[:, :, j * 64:(j + 1) * 64]), cvz[j], cv)
                        pgrp = None
                        for b in range(16):
                            cb = ckb[b % 2]
                            fw.dma("pool", lambda e, b=b, cb=cb: e.dma_start(out=cb[:], in_=ck[b]), cb, ck)
                            if b % 4 == 0:
                                pgrp = banks.next()
                            pbf = pgrp[:].bitcast(BF16)
                            fw.op("pe", lambda e, b=b, cb=cb, pbf=pbf: e.transpose(out=pbf[:, (b % 4) * 128:(b % 4 + 1) * 128], in_=cb[:], identity=identb[:]),
                                  reads=[cb, identb], writes=[pgrp])
                            if b % 4 == 3:
                                g0 = b - 3
                                fw.op("dve", lambda e, g0=g0, pbf=pbf: e.tensor_copy(out=ckT[:, g0:g0 + 4, :], in_=pbf[:, 0:512].rearrange("p (a b) -> p a b", a=4)), reads=[pgrp], writes=[ckT])
                        out_toks.append(fw.dma("sp", lambda e: e.dma_start(out=cks[:, 0:120, :], in_=ck[:, 8:128, :]), cks, ck))
                        out_toks.append(fw.dma("sp", lambda e: e.dma_start(out=cvs[:, 0:120, :], in_=cv[:, 8:128, :]), cvs, cv))
                        psc = [banks.next(), banks.next()]
                        for b in range(16):
                            for kv in range(2):
                                lo = kv * 64
                                p = psc[kv]
                                off = b * 32
                                fw.op("pe", lambda e, p=p, off=off, b=b, lo=lo: e.matmul(p[:, off:off + 32].rearrange("p (a q) -> p a q", a=4), lhsT=ckT[lo:lo + 64, b, :],
                                                                                      rhs=rqT[lo:lo + 64, :, b * 8:(b + 1) * 8], start=True, stop=True),
                                      reads=[ckT, rqT], writes=[p], inc=(b == 15))
                        for kv in range(2):
                            fw.op("act", lambda e, kv=kv: e.activation(out=PC[:, kv, :, :].rearrange("p b f -> p (b f)"), in_=psc[kv][:], func=AF.Exp, scale=0.125), reads=[psc[kv]], writes=[PC])
                        fw.op("dve", lambda e: e.tensor_tensor(out=PC[:].rearrange("p k b (a q) -> p (k b a) q", q=8), in0=PC[:].rearrange("p k b (a q) -> p (k b a) q", q=8),
                                                               in1=mcache[:].unsqueeze(1).to_broadcast([128, 128, 8]), op=ALU.mult), reads=[PC, mcache], writes=[PC])
                    pn = banks.next(); pd = banks.next()
                    seq = [(kv, bi) for kv in range(2) for bi in range(len(blocks))]
                    for i, (kv, bi) in enumerate(seq):
                        vb = blocks[bi][1][kv]
                        lastmm = (i == len(seq) - 1)
                        fw.op("pe", lambda e, vb=vb, kv=kv, bi=bi, i=i, lastmm=lastmm: e.matmul(pn[:], lhsT=vb[:], rhs=pts[(kv, bi)][:], start=(i == 0), stop=lastmm),
                              reads=[vb, pts[(kv, bi)]], writes=[pn], inc=lastmm)
                    if samp:
                        pnc = banks.next()
                        for b in range(16):
                            for kv in range(2):
                                fw.op("pe", lambda e, b=b, kv=kv: e.matmul(pnc[:, b * 32:(b + 1) * 32], lhsT=cvz[kv][:, b, :], rhs=PC[:, kv, b, :], start=(kv == 0), stop=(kv == 1)),
                                      reads=[cvz[kv], PC], writes=[pnc], inc=(b == 15 and kv == 1))
                        fw.op("act", lambda e: e.activation(out=numc[:], in_=pnc[:], func=AF.Copy), reads=[pnc], writes=[numc])
                    for i, (kv, bi) in enumerate(seq):
                        lastmm = (i == len(seq) - 1)
                        fw.op("pe", lambda e, kv=kv, bi=bi, i=i, lastmm=lastmm: e.matmul(pd[:], lhsT=oz[kv][:], rhs=pts[(kv, bi)][:], start=(i == 0), stop=lastmm),
                              reads=[oz[kv], pts[(kv, bi)]], writes=[pd], inc=lastmm)
                    fw.op("dve", lambda e: e.tensor_tensor(out=den[:].rearrange("p (a n) -> p a n", a=4), in0=pd[:].rearrange("p (a n) -> p a n", a=4),
                                                           in1=es[:, 0:4].unsqueeze(2).to_broadcast([128, 4, 128]), op=ALU.add), reads=[pd, es], writes=[den])
                    if samp:
                        pdc = banks.next()
                        for kv in range(2):
                            fw.op("pe", lambda e, kv=kv: e.matmul(pdc[:], lhsT=oz[kv][:], rhs=PC[:, kv, :, :].rearrange("p b f -> p (b f)"), start=(kv == 0), stop=(kv == 1)),
                                  reads=[oz[kv], PC], writes=[pdc], inc=(kv == 1))
                        fw.op("dve", lambda e: e.tensor_tensor(out=den[:].rearrange("p (a b q) -> p a b q", a=4, q=8), in0=den[:].rearrange("p (a b q) -> p a b q", a=4, q=8),
                                                               in1=pdc[:].rearrange("p (b a q) -> p a b q", a=4, q=8), op=ALU.add), reads=[den, pdc], writes=[den])
                    fw.op("dve", lambda e: e.reciprocal(out=den[:], in_=den[:]), reads=[den], writes=[den])
                    if samp:
                        fw.op("dve", lambda e: e.tensor_tensor(out=numc[:].rearrange("p (b a q) -> p a b q", a=4, q=8), in0=pn[:].rearrange("p (a b q) -> p a b q", a=4, q=8),
                                                               in1=numc[:].rearrange("p (b a q) -> p a b q", a=4, q=8), op=ALU.add), reads=[pn, numc], writes=[numc])
                        fw.op("dve", lambda e: e.tensor_tensor(out=aoT[:].rearrange("p a (b q) -> p a b q", q=8), in0=numc[:].rearrange("p (b a q) -> p a b q", a=4, q=8),
                                                               in1=den[:].rearrange("p (a b q) -> p a b q", a=4, q=8), op=ALU.mult), reads=[numc, den], writes=[aoT])
                    else:
                        fw.op("dve", lambda e: e.tensor_tensor(out=aoT[:].rearrange("p a n -> p (a n)"), in0=pn[:], in1=den[:], op=ALU.mult), reads=[pn, den], writes=[aoT])
                    if prew is not None and not samp:
                        prew.step(1)
                    pcs = []
                    for g in range(2):
                        pc = banks.next()
                        for dc in range(4):
                            col = (g * 4 + dc) * 128
                            for k in range(4):
                                fw.op("pe", lambda e, pc=pc, dc=dc, col=col, k=k: e.matmul(pc[:, dc * 128:(dc + 1) * 128], lhsT=wco[:, k, col:col + 128], rhs=bcT[:, k, :], start=(k == 0), stop=(k == 3)),
                                      reads=[wco, bcT], writes=[pc], inc=(dc == 3 and k == 3))
                        pcs.append(pc)
                    for g in range(2):
                        pc = pcs[g]
                        pa = banks.next()
                        for dc in range(4):
                            col = (g * 4 + dc) * 128
                            for h in range(4):
                                fw.op("pe", lambda e, pa=pa, dc=dc, col=col, h=h: e.matmul(pa[:, dc * 128:(dc + 1) * 128], lhsT=wao[:, h, col:col + 128], rhs=aoT[:, h, :], start=(h == 0), stop=(h == 3)),
                                      reads=[wao, aoT], writes=[pa], inc=(dc == 3 and h == 3))
                        fw.op("dve", lambda e, pc=pc, g=g: e.tensor_tensor(out=mt1[:], in0=v4(pc), in1=sgc[:, g * 4:(g + 1) * 4, :], op=ALU.mult), reads=[pc, sgc], writes=[mt1])
                        fw.op("dve", lambda e, pa=pa, g=g: e.tensor_tensor(out=mt2[:], in0=v4(pa), in1=sga[:, g * 4:(g + 1) * 4, :], op=ALU.mult), reads=[pa, sga], writes=[mt2])
                        fw.op("dve", lambda e, g=g: e.tensor_tensor(out=mT[:, g * 4:(g + 1) * 4, :], in0=mt1[:], in1=mt2[:], op=ALU.add), reads=[mt1, mt2], writes=[mT])
                    if ti + 1 < len(tile_list):
                        emit_xT(ti + 1)
                        emit_bch(ti + 1)
                    for hf in range(2):
                        p = banks.next()
                        for k in range(8):
                            fw.op("pe", lambda e, p=p, k=k, hf=hf: e.matmul(p[:], lhsT=mT[:, k, :], rhs=wo[:, k, hf * 512:(hf + 1) * 512], start=(k == 0), stop=(k == 7)),
                                  reads=[mT, wo], writes=[p], inc=(k == 7))
                        fw.op("dve", lambda e, p=p, hf=hf: e.scalar_tensor_tensor(out=y1[:, hf * 512:(hf + 1) * 512], in0=x[:, hf * 512:(hf + 1) * 512], scalar=ALPHA, in1=p[:], op0=ALU.mult, op1=ALU.add),
                              reads=[x, p], writes=[y1])
                    h = hh[0]
                    layer_norm(fw, y1, h, g1, b1, lnp)
                    fw.dma("pool", lambda e, h=h: e.dma_start(out=HS[t], in_=h[:]), HS, h, sem_buf=h)

                if prew is not None:
                    prew.handoff(False)
                    while prew.nc_ < 128:
                        prew.step(4)
                fw.barrier()
                fw.run_block()

        if "b1" in stages:
            with contextlib.ExitStack() as ps:
                fw.cur = ps
                wpq = fw.sbuf("b_wpq", [128, 8, 2048], BF16, dma=True)
                for k in range(8):
                    fw.dma("pool", lambda e, k=k: e.dma_start(out=wpq[:, k, :], in_=w_pq[k * 128:(k + 1) * 128, :]), wpq, w_pq)
                skb = fw.sbuf("b_skb", [128, 16, 128], BF16, dma=True)
                fw.dma("pool", lambda e: e.dma_start(out=skb[:], in_=skeys.t.rearrange("j k c -> k j c")), skb, skeys)
                identb = fw.sbuf("b_identb", [128, 128], BF16, dma=True)
                fw.dma("pool", lambda e: e.dma_start(out=identb[:], in_=c_ident[:]), identb, c_ident)
                identf = fw.sbuf("b_identf", [128, 128], F32, dma=True)
                fw.dma("sp", lambda e: e.dma_start(out=identf[:], in_=c_ident[:]), identf, c_ident)
                iota16 = fw.sbuf("b_iota16", [128, 16], F32, dma=True)
                fw.dma("sp", lambda e: e.dma_start(out=iota16[:], in_=c_iota[:, 0:16]), iota16, c_iota)
                KT = fw.sbuf("b_KT", [128, 16, 128], BF16)
                banks = Rot([fw.psum(f"b_ps{i}", [128, 512], F32) for i in range(8)])
                for g in range(4):
                    p = banks.next()
                    pbf = p[:].bitcast(BF16)
                    for jj in range(4):
                        fw.op("pe", lambda e, g=g, jj=jj, pbf=pbf: e.transpose(out=pbf[:, jj * 128:(jj + 1) * 128], in_=skb[:, g * 4 + jj, :], identity=identb[:]),
                              reads=[skb, identb], writes=[p], inc=(jj == 3))
                    fw.op("dve", lambda e, g=g, pbf=pbf: e.tensor_copy(out=KT[:, g * 4:(g + 1) * 4, :], in_=pbf[:, 0:512].rearrange("p (a b) -> p a b", a=4)), reads=[p], writes=[KT])
                hfb = [fw.sbuf(f"b_hf{i}", [128, 1024], F32, dma=True) for i in range(2)]
                hb_ = [fw.sbuf(f"b_hb{i}", [128, 1024], BF16) for i in range(2)]
                hT_ = [fw.sbuf(f"b_hT{i}", [128, 8, 128], BF16, dma=True) for i in range(2)]
                qvT_ = [fw.sbuf(f"b_qvT{i}", [128, 16, 128], BF16) for i in range(2)]
                sc_ = [fw.sbuf(f"b_sc{i}", [128, 16, 128], F32) for i in range(2)]
                work = fw.sbuf("b_work", [128, 16, 128], F32)
                tv = fw.sbuf("b_tv", [128, 16, 16], F32)
                ti = fw.sbuf("b_ti", [128, 16, 16], U32)
                tif = fw.sbuf("b_tif", [128, 16, 16], BF16)
                cand = fw.sbuf("b_cand", [128, 8, 256], F32)
                cwork = fw.sbuf("b_cwork", [128, 8, 256], F32)
                best = fw.sbuf("b_best", [128, 8, 16], F32)
                pos = fw.sbuf("b_pos", [128, 8, 16], U32)
                au = fw.sbuf("b_au", [128, 8, 16], U32)
                bu = fw.sbuf("b_bu", [128, 8, 16], U32)
                af_ = fw.sbuf("b_af", [128, 8, 16], BF16)
                bf_ = fw.sbuf("b_bf", [128, 8, 16], BF16)
                eq = fw.sbuf("b_eq", [128, 8, 16, 16], BF16)
                iota16b = fw.sbuf("b_iota16b", [128, 16], BF16)
                io128i = fw.sbuf("b_io128i", [128, 128], I32)
                io256i = fw.sbuf("b_io256i", [128, 256], I32)
                fw.op("pool", lambda e: e.iota(io128i[:], pattern=[[1, 128]], base=0, channel_multiplier=0), writes=[io128i])
                fw.op("pool", lambda e: e.iota(io256i[:], pattern=[[1, 256]], base=0, channel_multiplier=0), writes=[io256i])
                msk7 = fw.sbuf("b_msk7", [128, 1], I32)
                msk8 = fw.sbuf("b_msk8", [128, 1], I32)
                fw.op("pool", lambda e: e.memset(msk7[:], -128), writes=[msk7])
                fw.op("pool", lambda e: e.memset(msk8[:], -256), writes=[msk8])
                fw.op("dve", lambda e: e.tensor_copy(out=iota16b[:], in_=iota16[:]), reads=[iota16], writes=[iota16b])
                selv = fw.sbuf("b_selv", [128, 3, 8, 16], F32)
                ex = fw.sbuf("b_ex", [128, 8, 16], F32)
                zs = fw.sbuf("b_zs", [128, 8], F32)
                selT = fw.sbuf("b_selT", [128, 3, 128], F32, dma=True)
                tvA = [fw.alias(f"tvA{j}", tv[:, j, 0:8]) for j in range(16)]
                tvB = [fw.alias(f"tvB{j}", tv[:, j, 8:16]) for j in range(16)]
                tiA = [fw.alias(f"tiA{j}", ti[:, j, 0:8]) for j in range(16)]
                tiB = [fw.alias(f"tiB{j}", ti[:, j, 8:16]) for j in range(16)]
                workj = [fw.alias(f"workj{j}", work[:, j, :]) for j in range(16)]
                bsA = [fw.alias(f"bsA{h}", best[:, h, 0:8]) for h in range(8)]
                bsB = [fw.alias(f"bsB{h}", best[:, h, 8:16]) for h in range(8)]
                psA = [fw.alias(f"psA{h}", pos[:, h, 0:8]) for h in range(8)]
                psB = [fw.alias(f"psB{h}", pos[:, h, 8:16]) for h in range(8)]
                cworkh = [fw.alias(f"cworkh{h}", cwork[:, h, :]) for h in range(8)]

                def ldh(t, s):
                    fw.dma("sp", lambda e: e.dma_start(out=hfb[s][:], in_=HS[t]), hfb[s], HS)
                ldh(tile_list[0], 0)

                vstate = [0]

                def b1_vconv(cnt):
                    if "pre" not in stages or pre_alone:
                        return
                    for _ in range(cnt):
                        if vstate[0] >= 32:
                            return
                        i = vstate[0]; vstate[0] += 1
                        fw.dma("pool", lambda e: e.dma_start(out=VB[i * 512:(i + 1) * 512, :], in_=ev[i * 512:(i + 1) * 512, :]), VB, ev)

                def b1_front(ti_):
                    t = tile_list[ti_]
                    s = ti_ % 2
                    b1_vconv(1)
                    if ti_ + 1 < len(tile_list):
                        ldh(tile_list[ti_ + 1], 1 - s)
                    hf = hfb[s]
                    hb = hb_[s]; hT = hT_[s]; qvT = qvT_[s]; sc = sc_[s]
                    fw.op("act", lambda e: e.activation(out=hb[:], in_=hf[:], func=AF.Copy), reads=[hf], writes=[hb])
                    p = banks.next(); pbf = p[:].bitcast(BF16)
                    for k in range(8):
                        fw.op("pe", lambda e, k=k, pbf=pbf: e.transpose(out=pbf[:, k * 128:(k + 1) * 128], in_=hb[:, k * 128:(k + 1) * 128], identity=identb[:]),
                              reads=[hb, identb], writes=[p], inc=(k == 7))
                    fw.op("act", lambda e, pbf=pbf: e.activation(out=hT[:], in_=pbf[:].rearrange("p (a b) -> p a b", a=8), func=AF.Copy), reads=[p], writes=[hT])
                    fw.dma("sp", lambda e: e.dma_start(out=HT[t], in_=hT[:]), HT, hT, sem_buf=hT)
                    for g in range(4):
                        p = banks.next()
                        for jj in range(4):
                            j = g * 4 + jj
                            for k in range(8):
                                fw.op("pe", lambda e, p=p, jj=jj, j=j, k=k: e.matmul(p[:, jj * 128:(jj + 1) * 128], lhsT=wpq[:, k, j * 128:(j + 1) * 128], rhs=hT[:, k, :], start=(k == 0), stop=(k == 7)),
                                      reads=[wpq, hT], writes=[p], inc=(jj == 3 and k == 7))
                        fw.op("act", lambda e, p=p, g=g: e.activation(out=qvT[:, g * 4:(g + 1) * 4, :], in_=p[:].rearrange("p (a b) -> p a b", a=4), func=AF.Copy), reads=[p], writes=[qvT])
                    for g in range(4):
                        p = banks.next()
                        for jj in range(4):
                            j = g * 4 + jj
                            fw.op("pe", lambda e, p=p, jj=jj, j=j: e.matmul(p[:, jj * 128:(jj + 1) * 128], lhsT=qvT[:, j, :], rhs=KT[:, j, :], start=True, stop=True),
                                  reads=[qvT, KT], writes=[p], inc=(jj == 3))
                        fw.op("act", lambda e, p=p, g=g: e.activation(out=sc[:, g * 4:(g + 1) * 4, :], in_=p[:].rearrange("p (a b) -> p a b", a=4), func=AF.Copy), reads=[p], writes=[sc])

                def b1_back(ti_):
                    t = tile_list[ti_]
                    s = ti_ % 2
                    sc = sc_[s]
                    sci = sc[:].bitcast(I32)
                    fw.op("dve", lambda e: e.scalar_tensor_tensor(out=sci, in0=sci, scalar=msk7[:, 0:1], in1=io128i[:].unsqueeze(1).to_broadcast([128, 16, 128]),
                                                                  op0=ALU.bitwise_and, op1=ALU.bitwise_or), reads=[sc, msk7, io128i], writes=[sc])
                    for j in range(16):
                        fw.op("dve", lambda e, j=j: e.max(out=tvA[j][:], in_=sc[:, j, :]), reads=[sc], writes=[tvA[j]])
                    for j in range(16):
                        fw.op("dve", lambda e, j=j: e.match_replace(out=workj[j][:], in_to_replace=tvA[j][:], in_values=sc[:, j, :], imm_value=-1e30), reads=[sc, tvA[j]], writes=[workj[j]])
                    for j in range(16):
                        fw.op("dve", lambda e, j=j: e.max(out=tvB[j][:], in_=workj[j][:]), reads=[workj[j]], writes=[tvB[j]])
                    tv_deps = tvA + tvB
                    fw.op("dve", lambda e: e.tensor_scalar(out=ti[:].bitcast(I32), in0=tv[:].bitcast(I32), scalar1=127, scalar2=None, op0=ALU.bitwise_and), reads=tv_deps, writes=[ti])
                    ti_deps = [ti]
                    fw.op("dve", lambda e: e.tensor_copy(out=tif[:], in_=ti[:].bitcast(I32)), reads=ti_deps, writes=[tif])
                    tv4 = tv[:].rearrange("p (h t) k -> p h t k", t=2)
                    tif4 = tif[:].rearrange("p (h t) k -> p h t k", t=2)
                    fw.op("dve", lambda e: e.tensor_tensor(out=cand[:].rearrange("p h (a b) -> p h a b", a=16), in0=tv4[:, :, 0, :].unsqueeze(3).to_broadcast([128, 8, 16, 16]),
                                                           in1=tv4[:, :, 1, :].unsqueeze(2).to_broadcast([128, 8, 16, 16]), op=ALU.add), reads=tv_deps, writes=[cand])
                    cdi = cand[:].bitcast(I32)
                    fw.op("dve", lambda e: e.scalar_tensor_tensor(out=cdi, in0=cdi, scalar=msk8[:, 0:1], in1=io256i[:].unsqueeze(1).to_broadcast([128, 8, 256]),
                                                                  op0=ALU.bitwise_and, op1=ALU.bitwise_or), reads=[cand, msk8, io256i], writes=[cand])
                    for h in range(8):
                        fw.op("dve", lambda e, h=h: e.max(out=bsA[h][:], in_=cand[:, h, :]), reads=[cand], writes=[bsA[h]])
                    for h in range(8):
                        fw.op("dve", lambda e, h=h: e.match_replace(out=cworkh[h][:], in_to_replace=bsA[h][:], in_values=cand[:, h, :], imm_value=-1e30), reads=[cand, bsA[h]], writes=[cworkh[h]])
                    for h in range(8):
                        fw.op("dve", lambda e, h=h: e.max(out=bsB[h][:], in_=cworkh[h][:]), reads=[cworkh[h]], writes=[bsB[h]])
                    best_deps = bsA + bsB
                    fw.op("dve", lambda e: e.tensor_scalar(out=pos[:].bitcast(I32), in0=best[:].bitcast(I32), scalar1=255, scalar2=None, op0=ALU.bitwise_and), reads=best_deps, writes=[pos])
                    pos_deps = [pos]
                    fw.op("dve", lambda e: e.tensor_scalar(out=au[:], in0=pos[:], scalar1=4, scalar2=None, op0=ALU.logical_shift_right), reads=pos_deps, writes=[au])
                    fw.op("dve", lambda e: e.tensor_scalar(out=bu[:], in0=pos[:], scalar1=15, scalar2=None, op0=ALU.bitwise_and), reads=pos_deps, writes=[bu])
                    fw.op("dve", lambda e: e.tensor_copy(out=af_[:], in_=au[:]), reads=[au], writes=[af_])
                    fw.op("dve", lambda e: e.tensor_copy(out=bf_[:], in_=bu[:]), reads=[bu], writes=[bf_])
                    io4 = iota16b[:].unsqueeze(1).unsqueeze(1).to_broadcast([128, 8, 16, 16])
                    for which, srcf in ((0, af_), (1, bf_)):
                        fw.op("dve", lambda e, srcf=srcf: e.tensor_tensor(out=eq[:], in0=srcf[:].unsqueeze(3).to_broadcast([128, 8, 16, 16]), in1=io4, op=ALU.is_equal), reads=[srcf, iota16b], writes=[eq])
                        fw.op("dve", lambda e, which=which: e.tensor_tensor(out=eq[:], in0=eq[:], in1=tif4[:, :, which, :].unsqueeze(2).to_broadcast([128, 8, 16, 16]), op=ALU.mult), reads=[eq, tif], writes=[eq])
                        fw.op("dve", lambda e, which=which: e.tensor_reduce(out=selv[:, which, :, :], in_=eq[:], axis=AX.X, op=ALU.add), reads=[eq], writes=[selv])
                    fw.op("dve", lambda e: e.tensor_tensor(out=ex[:], in0=best[:], in1=best[:, :, 0:1].to_broadcast([128, 8, 16]), op=ALU.subtract), reads=best_deps, writes=[ex])
                    fw.op("act", lambda e: e.activation(out=ex[:], in_=ex[:], func=AF.Exp), reads=[ex], writes=[ex])
                    fw.op("dve", lambda e: e.tensor_reduce(out=zs[:], in_=ex[:], axis=AX.X, op=ALU.add), reads=[ex], writes=[zs])
                    fw.op("dve", lambda e: e.reciprocal(out=zs[:], in_=zs[:]), reads=[zs], writes=[zs])
                    fw.op("dve", lambda e: e.tensor_tensor(out=selv[:, 2, :, :], in0=ex[:], in1=zs[:].unsqueeze(2).to_broadcast([128, 8, 16]), op=ALU.mult), reads=[ex, zs], writes=[selv])
                    p = banks.next()
                    for w in range(3):
                        fw.op("pe", lambda e, p=p, w=w: e.transpose(out=p[:, w * 128:(w + 1) * 128], in_=selv[:, w, :, :].rearrange("p h k -> p (h k)"), identity=identf[:]),
                              reads=[selv, identf], writes=[p], inc=(w == 2))
                    fw.op("act", lambda e, p=p: e.activation(out=selT[:], in_=p[:, 0:384].rearrange("p (a b) -> p a b", a=3), func=AF.Copy), reads=[p], writes=[selT])
                    fw.dma("sp", lambda e: e.dma_start(out=SEL[t], in_=selT[:]), SEL, selT, sem_buf=selT)

                b1_front(0)
                for ti_ in range(len(tile_list)):
                    if ti_ + 1 < len(tile_list):
                        b1_front(ti_ + 1)
                    b1_back(ti_)
                b1_vconv(32)
                fw.barrier()
                fw.run_block()

        if "b2" in stages:
            with contextlib.ExitStack() as ps:
                fw.cur = ps
                TPT = 2
                GC = 2
                NSL = 3
                Gs = [fw.sbuf(f"c_G{i}", [128, TPT * 128, 128], BF16) for i in range(2)]
                sels = [fw.sbuf(f"c_sel{i}", [128, TPT, 3, 128], F32, dma=True) for i in range(2)]
                hTps = [fw.sbuf(f"c_hTp{i}", [128, 8, TPT * 128], BF16, dma=True) for i in range(2)]
                hfp = fw.sbuf("c_hfp", [128, TPT, 1024], F32, dma=True)
                iob = fw.sbuf("c_iob", [128, 128], BF16, dma=True)
                fw.dma("pool", lambda e: e.dma_start(out=iob[:], in_=c_iota[:]), iob, c_iota)
                g2 = fw.sbuf("c_g2", [128, 1024], F32, dma=True)
                b2 = fw.sbuf("c_b2", [128, 1024], F32, dma=True)
                fw.dma("sp", lambda e: e.dma_start(out=g2[:], in_=ln2g.t.partition_broadcast(128)), g2, ln2g)
                fw.dma("sp", lambda e: e.dma_start(out=b2[:], in_=ln2b.t.partition_broadcast(128)), b2, ln2b)
                eps = fw.sbuf("c_eps", [128, 1], F32)
                fw.op("pool", lambda e: e.memset(eps[:], LN_EPS), writes=[eps])
                rotA = Rot([fw.sbuf(f"c_A{i}", [128, 128], BF16) for i in range(8)])
                rotB = Rot([fw.sbuf(f"c_B{i}", [128, 128], BF16) for i in range(8)])
                uts = [fw.sbuf(f"c_ut{i}", [128, GC, 1024], BF16, dma=True) for i in range(NSL)]
                vbs = [fw.sbuf(f"c_vb{i}", [128, GC, 1024], BF16, dma=True) for i in range(NSL)]
                rotgl = Rot([fw.sbuf(f"c_gl{i}", [128, TPT * 128], BF16) for i in range(4)])
                rotco = Rot([fw.sbuf(f"c_co{i}", [128, TPT * 128], BF16) for i in range(4)])
                y2s = [fw.sbuf(f"c_y2{i}", [128, 1024], F32) for i in range(TPT)]
                ots = Rot([fw.sbuf(f"c_ot{i}", [128, 1024], F32, dma=True) for i in range(1)])
                lnp = dict(st6=fw.sbuf("c_st6", [128, 2, 6], F32), mv=fw.sbuf("c_mv", [128, 2], F32),
                           sd=fw.sbuf("c_sd", [128, 1], F32), rs=fw.sbuf("c_rs", [128, 1], F32),
                           nmr=fw.sbuf("c_nmr", [128, 1], F32), eps=eps)
                acc = [fw.psum(f"c_acc{i}", [128, 512], F32) for i in range(2 * TPT)]
                _pb = [fw.psum(f"c_pre{i}", [128, 512], F32) for i in range(4)]
                pre = Rot(_pb[0:3])
                gbk = Rot(_pb[3:4])
                _pt = [t_ for t_ in tile_list if t_ != NTILE - 1]
                passes = ([[NTILE - 1]] if (NTILE - 1) in tile_list else []) + [_pt[i:i + TPT] for i in range(0, len(_pt), TPT)]
                ngrp = 128 // GC

                def load_sel(pi, what=("sel", "ht")):
                    if pi >= len(passes):
                        return
                    tiles_p = passes[pi]; sel = sels[pi % 2]; hTp = hTps[pi % 2]
                    for tt, t in enumerate(tiles_p):
                        if "sel" in what:
                            fw.dma("sp", lambda e, tt=tt, t=t: e.dma_start(out=sel[:, tt, :, :], in_=SEL[t]), sel, SEL)
                        if "ht" in what:
                            fw.dma("sp", lambda e, tt=tt, t=t: e.dma_start(out=hTp[:, :, tt * 128:(tt + 1) * 128], in_=HT[t]), hTp, HT)

                class GB:
                    def __init__(self, pi):
                        self.G = Gs[pi % 2]; self.sel = sels[pi % 2]
                        self.ntok = len(passes[pi]) * 128
                        self.nd = 0
                        self.nm = 0
                        self.ab = {}
                        self.pg = None
                        self.pending = None

                    def dve(self, cnt):
                        for _ in range(cnt):
                            if self.nd >= self.ntok:
                                return
                            n = self.nd; self.nd += 1
                            tt, nn = divmod(n, 128)
                            A = rotA.next(); B = rotB.next(); sel = self.sel
                            fw.op("dve", lambda e: e.tensor_scalar(out=A[:], in0=iob[:], scalar1=sel[:, tt, 0, nn:nn + 1], scalar2=sel[:, tt, 2, nn:nn + 1], op0=ALU.is_equal, op1=ALU.mult),
                                  reads=[iob, sel], writes=[A])
                            fw.op("dve", lambda e: e.tensor_scalar(out=B[:], in0=iob[:], scalar1=sel[:, tt, 1, nn:nn + 1], scalar2=None, op0=ALU.is_equal),
                                  reads=[iob, sel], writes=[B])
                            self.ab[n] = (A, B)

                    def mm(self, cnt):
                        for _ in range(cnt):
                            if self.nm >= self.ntok:
                                return
                            n = self.nm
                            if n not in self.ab:
                                self.dve(1)
                            self.nm += 1
                            A, B = self.ab.pop(n)
                            if n % 4 == 0:
                                self.flush()
                                self.pg = gbk.next()
                            pg = self.pg; G = self.G
                            fw.op("pe", lambda e: e.matmul(pg[:, (n % 4) * 128:(n % 4 + 1) * 128], lhsT=B[:], rhs=A[:], start=True, stop=True),
                                  reads=[A, B], writes=[pg])
                            if n % 4 == 3:
                                self.flush()
                                self.pending = (pg, n)

                    def flush(self):
                        if self.pending is not None:
                            pg, n = self.pending; G = self.G
                            self.pending = None
                            fw.op("act", lambda e: e.activation(out=G[:, n - 3:n + 1, :], in_=pg[:].rearrange("p (a b) -> p a b", a=4), func=AF.Copy), reads=[pg], writes=[G])

                    def finish(self):
                        while self.nm < self.ntok:
                            self.dve(4); self.mm(4)
                        self.flush()

                def emit_ln(tt, t):
                    y2 = y2s[tt]
                    ot = ots.next()
                    layer_norm(fw, y2, ot, g2, b2, lnp, mul_eng="pool")
                    fw.dma("pool", lambda e: e.dma_start(out=yout[t], in_=ot[:]), yout, ot, sem_buf=ot)

                deferred = []
                load_sel(0)
                gb0 = GB(0)
                gb0.finish()
                for pi, tiles_p in enumerate(passes):
                    ntt = len(tiles_p)
                    ntok = ntt * 128
                    G = Gs[pi % 2]; hTp = hTps[pi % 2]
                    for tt, t in enumerate(tiles_p):
                        fw.dma("sp", lambda e, tt=tt, t=t: e.dma_start(out=hfp[:, tt, :], in_=HS[t]), hfp, HS)
                    gbn = None
                    if pi + 1 < len(passes):
                        if pi == 0:
                            load_sel(1, ("sel",))
                        load_sel(pi + 1, ("ht",))
                        gbn = GB(pi + 1)
                        gbn.dve(4)

                    gbase = pi * ngrp

                    def ldgrp(gg):
                        if gg >= len(passes) * ngrp:
                            return
                        g = gg % ngrp
                        u = uts[gg % NSL]; vb_ = vbs[gg % NSL]
                        fw.dma("sp", lambda e: e.dma_start(out=u[:], in_=UT[g * GC:(g + 1) * GC].rearrange("c p f -> p c f")), u, UT)
                        fw.dma("sp", lambda e: e.dma_start(out=vb_[:], in_=VB[g * GC * 128:(g + 1) * GC * 128, :].rearrange("(c p) f -> p c f", p=128)), vb_, VB)
                    if pi == 0:
                        for g_ in range(NSL):
                            ldgrp(g_)
                    LA = 2
                    pps = {}

                    def emit_pre(c):
                        g, cc = divmod(c, GC)
                        u = uts[(gbase + g) % NSL]
                        pp = pre.next()
                        pps[c] = pp
                        for k in range(8):
                            fw.op("pe", lambda e, k=k: e.matmul(pp[:, 0:ntok], lhsT=u[:, cc, k * 128:(k + 1) * 128], rhs=hTp[:, k, 0:ntok], start=(k == 0), stop=(k == 7)),
                                  reads=[u, hTp], writes=[pp], inc=(k == 7))

                    def emit_post(c):
                        g, cc = divmod(c, GC)
                        vb_ = vbs[(gbase + g) % NSL]
                        pp = pps.pop(c)
                        gl = rotgl.next(); co = rotco.next()
                        fw.op("act", lambda e: e.activation(out=gl[:, 0:ntok], in_=pp[:, 0:ntok], func=AF.Gelu), reads=[pp], writes=[gl])
                        fw.op("dve", lambda e: e.tensor_tensor(out=co[:, 0:ntok], in0=gl[:, 0:ntok], in1=G[:, 0:ntok, c], op=ALU.mult), reads=[gl, G], writes=[co])
                        for tt in range(ntt):
                            for h2 in range(2):
                                a = acc[tt * 2 + h2]
                                lastmm = (tt == ntt - 1 and h2 == 1)
                                fw.op("pe", lambda e, a=a, tt=tt, h2=h2: e.matmul(a[:], lhsT=co[:, tt * 128:(tt + 1) * 128], rhs=vb_[:, cc, h2 * 512:(h2 + 1) * 512], start=(c == 0), stop=(c == 127)),
                                      reads=[co, vb_], writes=[a], inc=lastmm)
                        if cc == GC - 1:
                            ldgrp(gbase + g + NSL)

                    for c in range(128 + LA):
                        if c < 128:
                            emit_pre(c)
                        if c - LA >= 0:
                            emit_post(c - LA)
                        if c in (8, 24) and deferred:
                            emit_ln(*deferred.pop(0))
                        if c == 64:
                            load_sel(pi + 2, ("sel",))
                        if gbn is not None:
                            gbn.flush()
                            gbn.dve(2)
                            gbn.mm(2)
                    if gbn is not None:
                        gbn.finish()
                    for tt, t in enumerate(tiles_p):
                        y2 = y2s[tt]
                        for h2 in range(2):
                            a = acc[tt * 2 + h2]
                            fw.op("dve", lambda e, a=a, tt=tt, h2=h2: e.scalar_tensor_tensor(out=y2[:, h2 * 512:(h2 + 1) * 512], in0=hfp[:, tt, h2 * 512:(h2 + 1) * 512], scalar=ALPHA, in1=a[:], op0=ALU.mult, op1=ALU.add),
                                  reads=[hfp, a], writes=[y2])
                    deferred = list(enumerate(tiles_p))
                for tt, t in deferred:
                    emit_ln(tt, t)
                fw.barrier()
                fw.run_block()

        fw.cur = st
        toks = [D[k].last_write for k in ("yout", "scp", "ckp", "cvp", "scs", "cks", "cvs")] + out_toks
        if debug:
            toks += [HS.last_write, SEL.last_write]
        fw.barrier()
        fw.wait_tok("sp", [t for t in toks if t is not None])
        fw.run_block()
    return nc


def _consts():
    half = 32
    inv_freq = (10000.0 ** (-2.0 * np.arange(half, dtype=np.float64) / 64.0)).astype(np.float32)
    pos = np.concatenate([np.arange(4096, dtype=np.float32), (16384 + (np.arange(128) % 8)).astype(np.float32)])
    ang = (pos[None, :] * inv_freq[:, None]).astype(np.float32)
    c = np.cos(ang.astype(np.float64)).astype(np.float32)
    s = np.sin(ang.astype(np.float64)).astype(np.float32)
    cos64 = np.concatenate([c, c], axis=0)
    sin64 = np.concatenate([-s, s], axis=0)
    ropec = np.ascontiguousarray(np.concatenate([cos64, cos64], axis=0))
    ropes = np.ascontiguousarray(np.concatenate([sin64, sin64], axis=0))
    ident = np.eye(128, dtype=np.float32)
    sidx = np.arange(128)[:, None]
    qidx = np.arange(128)[None, :]
    m_cur = np.where(sidx <= qidx, 0.0, NEG).astype(np.float32)
    m_prev = np.where(sidx > qidx, 0.0, NEG).astype(np.float32)
    m_samp = np.where((sidx <= qidx) & (sidx // 8 == qidx // 8), 0.0, NEG).astype(np.float32)
    masks = np.stack([np.tile(m, (1, 4)) for m in (m_cur, m_prev, m_samp)], axis=0)
    mcache = (np.arange(128)[:, None] > np.arange(8)[None, :]).astype(np.float32)
    iota = np.tile(np.arange(128, dtype=np.float32)[None, :], (128, 1))
    return dict(ropec=ropec, ropes=ropes, c_ident=ident, c_masks=np.ascontiguousarray(masks),
                c_mcache=np.ascontiguousarray(mcache), c_iota=iota)


def _wcols():
    cols = list(range(0, 1536))
    q0 = 1536
    for swap in (0, 32):
        for cp in range(4):
            for p in range(128):
                head = cp if p < 64 else 4 + cp
                i = p % 64
                cols.append(q0 + head * 64 + (i + swap) % 64)
    k0 = 2048
    for swap in (0, 32):
        for p in range(128):
            kv = p // 64
            i = p % 64
            cols.append(k0 + kv * 64 + (i + swap) % 64)
    cols += list(range(2176, 2304))
    cols += list(range(2304, 4352))
    assert len(cols) == WCOLS
    return np.array(cols)


def _aorows():
    rows = []
    for cp in range(4):
        for p in range(128):
            head = cp if p < 64 else 4 + cp
            rows.append(head * 64 + p % 64)
    return np.array(rows)


def make_in_maps(inp):
    C = _consts()
    f = lambda a: np.ascontiguousarray(np.asarray(a, dtype=np.float32))
    w_in_ext = f(np.asarray(inp["w_in"])[:, _wcols()])
    cw = f(np.asarray(inp["conv_w"]).reshape(3, 4, 128).transpose(2, 1, 0))
    shared = dict(w_in=w_in_ext, cw=cw, sinks=f(inp["attn_sinks"]), w_co=f(inp["w_conv_out"]), w_ao=f(np.asarray(inp["w_attn_out"])[_aorows()]),
                  w_o=f(inp["w_o"]), ln1g=f(inp["ln1_g"]), ln1b=f(inp["ln1_b"]), ln2g=f(inp["ln2_g"]), ln2b=f(inp["ln2_b"]),
                  w_pq=f(inp["w_pq"]), skeys=f(np.asarray(inp["sub_keys"]).reshape(16, 128, 128)),
                  eu=f(inp["expert_u"]), ev=f(inp["expert_v"]), **C)
    xp = np.asarray(inp["x_prompt"]); xs = np.asarray(inp["x_sample"])
    maps = []
    for i in range(NCORES):
        xin = np.concatenate([xp[i].reshape(32, 128, 1024), xs[16 * i:16 * i + 16].reshape(1, 128, 1024)], axis=0)
        m = dict(shared)
        m["xin"] = f(xin)
        m["sconv"] = f(np.asarray(inp["state_conv"])[16 * i:16 * i + 16].reshape(32, 512))
        m["ck"] = f(np.asarray(inp["cache_k"])[16 * i:16 * i + 16].reshape(16, 128, 128))
        m["cv"] = f(np.asarray(inp["cache_v"])[16 * i:16 * i + 16].reshape(16, 128, 128))
        maps.append(m)
    return maps


_NC_CACHE = {}


def kernel(**inputs):
    if "nc" not in _NC_CACHE:
        _NC_CACHE["nc"] = build_program()
    nc = _NC_CACHE["nc"]
    maps = make_in_maps(inputs)
    res = run_bass_kernel_spmd(nc, maps, core_ids=list(range(NCORES)))
    R = res.results
    y_prompt = np.stack([R[i]["yout"][:32].reshape(4096, 1024) for i in range(NCORES)], axis=0)
    y_sample = np.concatenate([R[i]["yout"][32].reshape(16, 8, 1024) for i in range(NCORES)], axis=0)
    scp = np.stack([R[i]["scp"] for i in range(NCORES)], axis=0)
    ckp = np.stack([R[i]["ckp"].reshape(128, 2, 64) for i in range(NCORES)], axis=0)
    cvp = np.stack([R[i]["cvp"].reshape(128, 2, 64) for i in range(NCORES)], axis=0)
    scs = np.concatenate([R[i]["scs"] for i in range(NCORES)], axis=0)
    cks = np.concatenate([R[i]["cks"].reshape(16, 128, 2, 64) for i in range(NCORES)], axis=0)
    cvs = np.concatenate([R[i]["cvs"].reshape(16, 128, 2, 64) for i in range(NCORES)], axis=0)
    return (y_prompt.astype(np.float32), y_sample.astype(np.float32), scp.astype(np.float32), ckp.astype(np.float32),
            cvp.astype(np.float32), scs.astype(np.float32), cks.astype(np.float32), cvs.astype(np.float32))
```

```python
import contextlib
import os
import numpy as np
import ml_dtypes
import concourse.bass as bass
import concourse.mybir as mybir
from concourse.bass_utils import run_bass_kernel_spmd

F32 = mybir.dt.float32
BF16 = mybir.dt.bfloat16
I32 = mybir.dt.int32
U32 = mybir.dt.uint32
AF = mybir.ActivationFunctionType
ALU = mybir.AluOpType
AX = mybir.AxisListType

NCORES = 8
NT_P = 32
NTILE = 33
ALPHA = float((2.0 * 1) ** 0.25)
LN_EPS = 1e-5
NEG = -30000.0
WCOLS = 4992


class Buf:
    def __init__(self, name, t=None, sem=None):
        self.name = name
        self.t = t
        self.sem = sem
        self.dma_total = 0
        self.last_write = None
        self.reads = []

    def __getitem__(self, idx):
        return self.t[idx]


class _Rec:
    def __getattr__(self, name):
        def f(*a, **k):
            return (name, a, k)
        return f


REC = _Rec()


class Eng:
    def __init__(self, name, sem):
        self.name = name
        self.sem = sem
        self.count = 0
        self.known = {}
        self.prog = []


class FW:
    def __init__(self, nc, stack):
        self.nc = nc
        self.stack = stack
        self.cur = stack
        self.E = {}
        self.sem_bufs = []
        for name in ("pe", "act", "dve", "pool", "sp"):
            self.E[name] = Eng(name, self.new_sem("s_" + name))

    def new_sem(self, name):
        return self.stack.enter_context(self.nc.semaphore(name))

    def sbuf(self, name, shape, dtype, dma=False):
        t = self.cur.enter_context(self.nc.sbuf_tensor(name, shape, dtype))
        b = Buf(name, t, self.new_sem("m_" + name) if dma else None)
        if dma:
            self.sem_bufs.append(b)
        return b

    def psum(self, name, shape, dtype):
        t = self.cur.enter_context(self.nc.psum_tensor(name, shape, dtype))
        return Buf(name, t, None)

    def dram(self, name, shape, dtype, kind):
        t = self.nc.dram_tensor(name, shape, dtype, kind=kind)
        if kind == "ExternalInput":
            return Buf(name, t.ap(), None)
        b = Buf(name, t.ap(), self.new_sem("m_" + name))
        self.sem_bufs.append(b)
        return b

    def alias(self, name, t, dma=False):
        b = Buf(name, t, self.new_sem("m_" + name) if dma else None)
        if dma:
            self.sem_bufs.append(b)
        return b

    def _need(self, E, waits, tok):
        if tok is None:
            return
        sem, val = tok
        if E.name == "pe" and sem is E.sem:
            return
        key = id(sem)
        if E.known.get(key, 0) >= val:
            return
        if key not in waits or waits[key][1] < val:
            waits[key] = (sem, val)

    def op(self, ename, fn, reads=(), writes=(), inc=True):
        fn = fn(REC)
        E = self.E[ename]
        waits = {}
        for b in reads:
            self._need(E, waits, b.last_write)
        for b in writes:
            self._need(E, waits, b.last_write)
            for t in b.reads:
                self._need(E, waits, t)
        for key, (sem, val) in waits.items():
            E.known[key] = val
        if inc:
            E.count += 1
            tok = (E.sem, E.count)
        else:
            tok = (E.sem, E.count + 1)
        for b in reads:
            b.reads.append(tok)
            if len(b.reads) > 64:
                b.reads = b.reads[-48:] if False else b.reads
        for b in writes:
            b.last_write = tok
            b.reads = []
        E.prog.append((list(waits.values()), fn, (E.sem, 1) if inc else None))

    def dma(self, qname, fn, dst, src, sem_buf=None):
        fn = fn(REC)
        E = self.E[qname]
        sb = sem_buf if sem_buf is not None else dst
        assert sb.sem is not None, sb.name
        waits = {}
        self._need(E, waits, src.last_write)
        lw = dst.last_write
        if lw is not None and lw[0] is not sb.sem:
            self._need(E, waits, lw)
        for t in dst.reads:
            self._need(E, waits, t)
        for key, (sem, val) in waits.items():
            E.known[key] = val
        sb.dma_total += 16
        tok = (sb.sem, sb.dma_total)
        src.reads.append(tok)
        dst.last_write = tok
        dst.reads = []
        E.prog.append((list(waits.values()), fn, (sb.sem, 16)))
        return tok

    def wait_tok(self, ename, toks):
        E = self.E[ename]
        waits = {}
        for t in toks:
            self._need(E, waits, t)
        for key, (sem, val) in waits.items():
            E.known[key] = val
        E.prog.append((list(waits.values()), None, None))

    def barrier(self):
        toks = []
        for E in self.E.values():
            if E.count > 0:
                toks.append((E.sem, E.count))
        for b in self.sem_bufs:
            if b.dma_total > 0:
                toks.append((b.sem, b.dma_total))
        for name in self.E:
            self.wait_tok(name, toks)

    def run_block(self):
        nc = self.nc
        with nc.Block() as block:
            def mk(E):
                def body(eng):
                    for waits, fn, inc in E.prog:
                        for sem, val in waits:
                            eng.wait_ge(sem, val)
                        if fn is None:
                            continue
                        ins = getattr(eng, fn[0])(*fn[1], **fn[2])
                        if inc is not None:
                            ins.then_inc(inc[0], inc[1])
                return body
            block.tensor(mk(self.E["pe"]))
            block.scalar(mk(self.E["act"]))
            block.vector(mk(self.E["dve"]))
            block.gpsimd(mk(self.E["pool"]))
            block.sync(mk(self.E["sp"]))
        for E in self.E.values():
            E.prog = []


class Rot:
    def __init__(self, bufs):
        self.bufs = bufs
        self.i = 0

    def next(self):
        b = self.bufs[self.i % len(self.bufs)]
        self.i += 1
        return b


def bcast(ap, axis, shape):
    return ap.unsqueeze(axis).to_broadcast(shape)


def layer_norm(fw, y, out, g_rep, b_rep, tmp_pool, mul_eng="dve"):
    st6, mv, sd, rs, nmr = (tmp_pool[k] for k in ("st6", "mv", "sd", "rs", "nmr"))
    for hf in range(2):
        fw.op("dve", lambda e, hf=hf: e.bn_stats(out=st6[:, hf, :], in_=y[:, hf * 512:(hf + 1) * 512]),
              reads=[y], writes=[st6])
    fw.op("dve", lambda e: e.bn_aggr(out=mv[:], in_=st6[:].rearrange("p a b -> p (a b)")), reads=[st6], writes=[mv])
    fw.op("act", lambda e: e.activation(out=sd[:], in_=mv[:, 1:2], func=AF.Sqrt, bias=tmp_pool["eps"][:], scale=1.0),
          reads=[mv, tmp_pool["eps"]], writes=[sd])
    fw.op("dve", lambda e: e.reciprocal(out=rs[:], in_=sd[:]), reads=[sd], writes=[rs])
    fw.op("dve", lambda e: e.tensor_scalar(out=nmr[:], in0=mv[:, 0:1], scalar1=rs[:, 0:1], scalar2=-1.0,
                                           op0=ALU.mult, op1=ALU.mult), reads=[mv, rs], writes=[nmr])
    fw.op("act", lambda e: e.activation(out=y[:], in_=y[:], func=AF.Identity, bias=nmr[:, 0:1], scale=rs[:, 0:1]),
          reads=[y, nmr, rs], writes=[y])
    fw.op(mul_eng, lambda e: e.tensor_tensor(out=y[:], in0=y[:], in1=g_rep[:], op=ALU.mult), reads=[y, g_rep], writes=[y])
    fw.op("pool", lambda e: e.tensor_tensor(out=out[:], in0=y[:], in1=b_rep[:], op=ALU.add), reads=[y, b_rep], writes=[out])


def build_program(stages=("pre", "a", "b1", "b2"), debug=False, tile_list=None):
    tile_list = list(range(NTILE)) if tile_list is None else list(tile_list)
    nc = bass.Bass("TRN2", target_bir_lowering=False)
    with contextlib.ExitStack() as st:
        fw = FW(nc, st)
        D = {}

        def din(name, shape, dtype=F32):
            D[name] = fw.dram(name, shape, dtype, "ExternalInput")
            return D[name]

        def dout(name, shape, dtype=F32):
            D[name] = fw.dram(name, shape, dtype, "ExternalOutput")
            return D[name]

        dbg = "ExternalOutput" if debug else "Internal"

        xin = din("xin", [NTILE, 128, 1024])
        sconv = din("sconv", [32, 512])
        ck = din("ck", [16, 128, 128])
        cv = din("cv", [16, 128, 128])
        w_in = din("w_in", [1024, WCOLS])
        cw_in = din("cw", [128, 4, 3])
        sinks = din("sinks", [8])
        w_co = din("w_co", [512, 1024])
        w_ao = din("w_ao", [512, 1024])
        w_o = din("w_o", [1024, 1024])
        ln1g = din("ln1g", [1024]); ln1b = din("ln1b", [1024])
        ln2g = din("ln2g", [1024]); ln2b = din("ln2b", [1024])
        w_pq = din("w_pq", [1024, 2048])
        skeys = din("skeys", [16, 128, 128])
        eu = din("eu", [16384, 1024])
        ev = din("ev", [16384, 1024])
        ropec = din("ropec", [128, NTILE * 128])
        ropes = din("ropes", [128, NTILE * 128])
        c_ident = din("c_ident", [128, 128])
        c_masks = din("c_masks", [3, 128, 512])
        c_mcache = din("c_mcache", [128, 8])
        c_iota = din("c_iota", [128, 128])

        yout = dout("yout", [NTILE, 128, 1024])
        scp = dout("scp", [2, 512])
        ckp = dout("ckp", [128, 128])
        cvp = dout("cvp", [128, 128])
        scs = dout("scs", [16, 2, 512])
        cks = dout("cks", [16, 128, 128])
        cvs = dout("cvs", [16, 128, 128])

        UT = fw.dram("UT", [128, 128, 1024], BF16, "Internal")
        VB = fw.dram("VB", [16384, 1024], BF16, "Internal")
        HS = fw.dram("HS", [NTILE, 128, 1024], F32, dbg)
        HT = fw.dram("HT", [NTILE, 128, 8, 128], BF16, "Internal")
        SEL = fw.dram("SEL", [NTILE, 128, 3, 128], F32, dbg)

        out_toks = []

        pre_alone = ("a" not in stages) or bool(os.environ.get("PRE_STANDALONE"))
        if "pre" in stages and pre_alone:
            with contextlib.ExitStack() as ps:
                fw.cur = ps
                identb = fw.sbuf("p_identb", [128, 128], BF16, dma=True)
                fw.dma("pool", lambda e: e.dma_start(out=identb[:], in_=c_ident[:]), identb, c_ident)
                for i in range(32):
                    fw.dma("pool", lambda e, i=i: e.dma_start(out=VB[i * 512:(i + 1) * 512, :], in_=ev[i * 512:(i + 1) * 512, :]), VB, ev)
                ub = [fw.sbuf(f"p_ub{i}", [128, 1024], BF16, dma=True) for i in range(3)]
                utb = [fw.sbuf(f"p_utb{i}", [128, 1024], BF16, dma=True) for i in range(3)]
                pb = [fw.psum(f"p_ps{i}", [128, 1024], BF16) for i in range(4)]
                for c in range(128):
                    u = ub[c % 3]; ut = utb[c % 3]; p = pb[c % 4]
                    fw.dma("pool", lambda e, c=c, u=u: e.dma_start(out=u[:], in_=eu[c * 128:(c + 1) * 128, :]), u, eu)
                    for k in range(8):
                        fw.op("pe", lambda e, k=k, u=u, p=p: e.transpose(out=p[:, k * 128:(k + 1) * 128], in_=u[:, k * 128:(k + 1) * 128], identity=identb[:]),
                              reads=[u, identb], writes=[p], inc=(k == 7))
                    if c % 2 == 0:
                        fw.op("act", lambda e, ut=ut, p=p: e.activation(out=ut[:], in_=p[:], func=AF.Copy), reads=[p], writes=[ut])
                    else:
                        fw.op("dve", lambda e, ut=ut, p=p: e.tensor_copy(out=ut[:], in_=p[:]), reads=[p], writes=[ut])
                    fw.dma("sp", lambda e, c=c, ut=ut: e.dma_start(out=UT[c], in_=ut[:]), UT, ut, sem_buf=ut)
                fw.barrier()
                fw.run_block()

        if "a" in stages:
            with contextlib.ExitStack() as ps:
                fw.cur = ps
                wi = fw.sbuf("a_wi", [128, 8, WCOLS], BF16, dma=False)
                WG = [(0, 12), (12, 23), (23, 39)]
                wig = [fw.alias(f"a_wig{i}", wi.t, dma=True) for i in range(3)]
                for gi, (c0_, c1_) in enumerate(WG):
                    for k in range(8):
                        fw.dma("pool", lambda e, k=k, c0_=c0_, c1_=c1_: e.dma_start(out=wi[:, k, c0_ * 128:c1_ * 128], in_=w_in[k * 128:(k + 1) * 128, c0_ * 128:c1_ * 128]), wig[gi], w_in)

                def wbuf(chunk):
                    return wig[0] if chunk < 12 else (wig[1] if chunk < 23 else wig[2])
                wco = fw.sbuf("a_wco", [128, 4, 1024], BF16, dma=True)
                fw.dma("pool", lambda e: e.dma_start(out=wco[:], in_=w_co.t.rearrange("(k p) c -> p k c", p=128)), wco, w_co)
                wao = fw.sbuf("a_wao", [128, 4, 1024], BF16, dma=True)
                fw.dma("pool", lambda e: e.dma_start(out=wao[:], in_=w_ao.t.rearrange("(k p) c -> p k c", p=128)), wao, w_ao)
                wo = fw.sbuf("a_wo", [128, 8, 1024], BF16, dma=True)
                fw.dma("pool", lambda e: e.dma_start(out=wo[:], in_=w_o.t.rearrange("(k p) c -> p k c", p=128)), wo, w_o)
                cwt = fw.sbuf("a_cw", [128, 4, 3], F32, dma=True)
                fw.dma("sp", lambda e: e.dma_start(out=cwt[:], in_=cw_in[:]), cwt, cw_in)
                es = fw.sbuf("a_es", [128, 4], F32, dma=True)
                fw.dma("sp", lambda e: e.dma_start(out=es[0:64, :], in_=sinks.t[0:4].partition_broadcast(64)), es, sinks)
                fw.dma("sp", lambda e: e.dma_start(out=es[64:128, :], in_=sinks.t[4:8].partition_broadcast(64)), es, sinks)
                fw.op("act", lambda e: e.activation(out=es[:], in_=es[:], func=AF.Exp), reads=[es], writes=[es])
                g1 = fw.sbuf("a_g1", [128, 1024], F32, dma=True)
                b1 = fw.sbuf("a_b1", [128, 1024], F32, dma=True)
                fw.dma("sp", lambda e: e.dma_start(out=g1[:], in_=ln1g.t.partition_broadcast(128)), g1, ln1g)
                fw.dma("sp", lambda e: e.dma_start(out=b1[:], in_=ln1b.t.partition_broadcast(128)), b1, ln1b)
                identf = fw.sbuf("a_identf", [128, 128], F32, dma=True)
                fw.dma("sp", lambda e: e.dma_start(out=identf[:], in_=c_ident[:]), identf, c_ident)
                identb = fw.sbuf("a_identb", [128, 128], BF16, dma=True)
                fw.dma("pool", lambda e: e.dma_start(out=identb[:], in_=c_ident[:]), identb, c_ident)
                masks = fw.sbuf("a_masks", [128, 3, 512], BF16, dma=True)
                fw.dma("pool", lambda e: e.dma_start(out=masks[:], in_=c_masks.t.rearrange("m p f -> p m f")), masks, c_masks)
                mcache = fw.sbuf("a_mcache", [128, 8], BF16, dma=True)
                fw.dma("pool", lambda e: e.dma_start(out=mcache[:], in_=c_mcache[:]), mcache, c_mcache)
                oz = [fw.sbuf(f"a_oz{i}", [128, 128], BF16) for i in range(2)]
                for i in range(2):
                    fw.op("pool", lambda e, i=i: e.memset(oz[i][:], 0.0), writes=[oz[i]])
                    fw.op("pool", lambda e, i=i: e.memset(oz[i][:, i * 64:(i + 1) * 64], 1.0), writes=[oz[i]])
                eps = fw.sbuf("a_eps", [128, 1], F32)
                fw.op("pool", lambda e: e.memset(eps[:], LN_EPS), writes=[eps])

                xt = [fw.sbuf(f"a_xt{i}", [128, 1024], F32, dma=True) for i in range(2)]
                cosb = [fw.sbuf(f"a_cos{i}", [128, 128], F32, dma=True) for i in range(2)]
                sinb = [fw.sbuf(f"a_sin{i}", [128, 128], F32, dma=True) for i in range(2)]
                xTs = [fw.sbuf(f"a_xT{i}", [128, 8, 128], BF16) for i in range(2)]
                bT = fw.sbuf("a_bT", [128, 4, 128], F32)
                cT = fw.sbuf("a_cT", [128, 4, 128], F32)
                ubp = fw.sbuf("a_ubp", [128, 4, 1, 130], F32)
                ubs = fw.sbuf("a_ubs", [128, 4, 16, 10], F32)
                fw.op("pool", lambda e: e.memset(ubp[:], 0.0), writes=[ubp])
                cy = fw.sbuf("a_cy", [128, 4, 128], F32)
                ct1 = fw.sbuf("a_ct1", [128, 4, 128], F32)
                tq1 = fw.sbuf("a_tq1", [128, 4, 128], F32)
                tq2 = fw.sbuf("a_tq2", [128, 4, 128], F32)
                mt1 = tq1; mt2 = tq2
                bcT = fw.sbuf("a_bcT", [128, 4, 128], BF16)
                rqT = fw.sbuf("a_rqT", [128, 4, 128], BF16)
                tk1 = fw.sbuf("a_tk1", [128, 128], F32)
                tk2 = fw.sbuf("a_tk2", [128, 128], F32)
                rkT = [fw.sbuf(f"a_rkT{i}", [128, 128], BF16) for i in range(2)]
                rkf = fw.sbuf("a_rkf", [128, 128], F32)
                vz = [[fw.sbuf(f"a_vz{i}{j}", [128, 128], BF16) for j in range(2)] for i in range(2)]
                for i in range(2):
                    for j in range(2):
                        fw.op("pool", lambda e, i=i, j=j: e.memset(vz[i][j][:], 0.0), writes=[vz[i][j]])
                vf = fw.sbuf("a_vf", [128, 128], F32, dma=True)
                kf = fw.sbuf("a_kf", [128, 128], F32, dma=True)
                uf = fw.sbuf("a_uf", [128, 512], F32, dma=True)
                sgc = fw.sbuf("a_sgc", [128, 8, 128], F32)
                sga = fw.sbuf("a_sga", [128, 8, 128], F32)
                PT = [fw.sbuf(f"a_PT{i}", [128, 512], BF16) for i in range(4)]
                den = fw.sbuf("a_den", [128, 512], F32)
                aoT = fw.sbuf("a_aoT", [128, 4, 128], BF16)
                mT = fw.sbuf("a_mT", [128, 8, 128], BF16)
                y1 = fw.sbuf("a_y1", [128, 1024], F32)
                hh = [fw.sbuf(f"a_h{i}", [128, 1024], F32, dma=True) for i in range(1)]
                lnp = dict(st6=fw.sbuf("a_st6", [128, 2, 6], F32), mv=fw.sbuf("a_mv", [128, 2], F32),
                           sd=fw.sbuf("a_sd", [128, 1], F32), rs=fw.sbuf("a_rs", [128, 1], F32),
                           nmr=fw.sbuf("a_nmr", [128, 1], F32), eps=eps)
                scl = fw.sbuf("a_scl", [32, 512], F32, dma=True)
                ckb = [fw.sbuf(f"a_ckb{i}", [128, 128], BF16, dma=True) for i in range(2)]
                cvz = [fw.sbuf(f"a_cvz{i}", [128, 16, 128], BF16, dma=True) for i in range(2)]
                ckT = fw.sbuf("a_ckT", [128, 16, 128], BF16)
                PC = fw.sbuf("a_PC", [128, 2, 16, 32], BF16)
                numc = fw.sbuf("a_numc", [128, 512], F32)

                banks = Rot([fw.psum(f"a_ps{i}", [128, 512], F32) for i in range(8)])

                class PreWork:
                    def __init__(self):
                        self.ub = [fw.alias(f"a_pub{i}", cvz[i][:].rearrange("p a b -> p (a b)").bitcast(F32), dma=True) for i in range(2)]
                        v2 = lambda b_, h_: b_[:, h_ * 8:(h_ + 1) * 8, :].rearrange("p a b -> p (a b)")
                        self.utb = [fw.alias(f"a_putb{i}", v2(ckT, i), dma=True) for i in range(2)]
                        self.shared = [(self.ub[0], cvz[0]), (self.ub[1], cvz[1]), (self.utb[0], ckT), (self.utb[1], ckT)]
                        self.nl = 0
                        self.nc_ = 0
                        self.nv = 0

                    def handoff(self, to_sample):
                        for ab, tb in self.shared:
                            src_, dst_ = (ab, tb) if to_sample else (tb, ab)
                            dst_.reads = dst_.reads + src_.reads + ([src_.last_write] if src_.last_write is not None else [])

                    def load(self, cnt):
                        for _ in range(cnt):
                            if self.nl >= 128:
                                return
                            c = self.nl; self.nl += 1
                            u = self.ub[c % 2]
                            fw.dma("sp", lambda e: e.dma_start(out=u[:], in_=eu[c * 128:(c + 1) * 128, :]), u, eu)

                    def vconv(self, cnt):
                        for _ in range(cnt):
                            if self.nv >= 32:
                                return
                            i = self.nv; self.nv += 1
                            fw.dma("pool", lambda e: e.dma_start(out=VB[i * 512:(i + 1) * 512, :], in_=ev[i * 512:(i + 1) * 512, :]), VB, ev)

                    def step(self, cnt):
                        for _ in range(cnt):
                            if self.nc_ >= 128:
                                return
                            c = self.nc_; self.nc_ += 1
                            while self.nl <= c:
                                self.load(1)
                            u = self.ub[c % 2]; ut = self.utb[c % 2]
                            pq = [banks.next(), banks.next()]
                            for k in range(8):
                                p = pq[k // 4]
                                fw.op("pe", lambda e, k=k, p=p: e.transpose(out=p[:, (k % 4) * 128:(k % 4 + 1) * 128], in_=u[:, k * 128:(k + 1) * 128], identity=identf[:]),
                                      reads=[u, identf], writes=[p], inc=(k % 4 == 3))
                            self.load(1)
                            for hf_ in range(2):
                                fw.op("act", lambda e, hf_=hf_: e.activation(out=ut[:, hf_ * 512:(hf_ + 1) * 512], in_=pq[hf_][:], func=AF.Copy), reads=[pq[hf_]], writes=[ut])
                            fw.dma("sp", lambda e: e.dma_start(out=UT[c], in_=ut[:]), UT, ut, sem_buf=ut)

                prew = PreWork() if ("pre" in stages and not pre_alone) else None
                if prew is not None:
                    prew.load(2)

                def load_tile(t, s):
                    fw.dma("sp", lambda e: e.dma_start(out=xt[s][:], in_=xin[t]), xt[s], xin)
                    fw.dma("sp", lambda e: e.dma_start(out=cosb[s][:], in_=ropec[:, t * 128:(t + 1) * 128]), cosb[s], ropec)
                    fw.dma("sp", lambda e: e.dma_start(out=sinb[s][:], in_=ropes[:, t * 128:(t + 1) * 128]), sinb[s], ropes)

                load_tile(tile_list[0], 0)

                def emit_xT(ti):
                    s = ti % 2
                    x = xt[s]; xT = xTs[s]
                    for hf in range(2):
                        p = banks.next()
                        for kk in range(4):
                            k = hf * 4 + kk
                            fw.op("pe", lambda e, k=k, kk=kk, p=p: e.transpose(out=p[:, kk * 128:(kk + 1) * 128], in_=x[:, k * 128:(k + 1) * 128], identity=identf[:]),
                                  reads=[x, identf], writes=[p], inc=(kk == 3))
                        if hf == 0:
                            fw.op("act", lambda e, hf=hf, p=p: e.activation(out=xT[:, hf * 4:(hf + 1) * 4, :], in_=p[:].rearrange("p (a b) -> p a b", a=4), func=AF.Copy), reads=[p], writes=[xT])
                        else:
                            fw.op("dve", lambda e, hf=hf, p=p: e.tensor_copy(out=xT[:, hf * 4:(hf + 1) * 4, :], in_=p[:].rearrange("p (a b) -> p a b", a=4)), reads=[p], writes=[xT])

                def proj(xT, c0, nch):
                    p = banks.next()
                    for j in range(nch):
                        for k in range(8):
                            fw.op("pe", lambda e, j=j, k=k, p=p: e.matmul(p[:, j * 128:(j + 1) * 128], lhsT=wi[:, k, (c0 + j) * 128:(c0 + j + 1) * 128], rhs=xT[:, k, :], start=(k == 0), stop=(k == 7)),
                                  reads=[wbuf(c0), xT], writes=[p], inc=(j == nch - 1 and k == 7))
                    return p

                def v4(p):
                    return p[:].rearrange("p (a b) -> p a b", a=4)

                def emit_bch(ti):
                    t = tile_list[ti]
                    samp = (t == NTILE - 1)
                    xT = xTs[ti % 2]
                    p = proj(xT, 0, 4)
                    fw.op("act", lambda e, p=p: e.activation(out=bT[:], in_=v4(p), func=AF.Copy), reads=[p], writes=[bT])
                    p = proj(xT, 4, 4)
                    fw.op("act", lambda e, p=p: e.activation(out=cT[:], in_=v4(p), func=AF.Copy), reads=[p], writes=[cT])
                    p = proj(xT, 8, 4)
                    ub = ubs if samp else ubp
                    if samp:
                        udst = lambda: ubs[:, :, :, 2:10]
                        uin = lambda ap: ap.rearrange("p a (b t) -> p a b t", t=8)
                    else:
                        udst = lambda: ubp[:, :, 0, 2:130]
                        uin = lambda ap: ap
                    fw.op("dve", lambda e, p=p: e.tensor_tensor(out=udst(), in0=uin(v4(p)), in1=uin(cT[:]), op=ALU.mult), reads=[p, cT], writes=[ub])
                    if samp:
                        fw.dma("sp", lambda e: e.dma_start(out=scl[:], in_=sconv[:]), scl, sconv)
                        p = banks.next()
                        for c in range(4):
                            fw.op("pe", lambda e, c=c, p=p: e.transpose(out=p[:, c * 32:(c + 1) * 32], in_=scl[:, c * 128:(c + 1) * 128], identity=identf[0:32, 0:32]),
                                  reads=[scl, identf], writes=[p], inc=(c == 3))
                        fw.op("act", lambda e, p=p: e.activation(out=ubs[:, :, :, 0:2], in_=p[:, 0:128].rearrange("p (a b j) -> p a b j", a=4, j=2), func=AF.Copy), reads=[p], writes=[ubs])

                for ti, t in enumerate(tile_list):
                    samp = (t == NTILE - 1)
                    s = ti % 2
                    if samp and prew is not None:
                        while prew.nc_ < 128:
                            prew.step(4)
                        prew.handoff(True)
                    if ti + 1 < len(tile_list):
                        load_tile(tile_list[ti + 1], 1 - s)
                    x = xt[s]; cs = cosb[s]; sn = sinb[s]
                    xT = xTs[s]
                    if ti == 0:
                        emit_xT(0)

                    ub = ubs if samp else ubp
                    if ti == 0:
                        emit_bch(0)
                    if prew is not None and not samp:
                        prew.step(1)
                    if samp:
                        usl = lambda o: ubs[:, :, :, o:o + 8]
                        cwb = lambda j: cwt[:, :, j:j + 1].unsqueeze(3).to_broadcast([128, 4, 16, 8])
                        v = lambda ap: ap.rearrange("p a (b t) -> p a b t", t=8)
                    else:
                        usl = lambda o: ubp[:, :, 0, o:o + 128]
                        cwb = lambda j: cwt[:, :, j:j + 1].to_broadcast([128, 4, 128])
                        v = lambda ap: ap
                    fw.op("pool", lambda e: e.tensor_tensor(out=v(cy[:]), in0=usl(2), in1=cwb(2), op=ALU.mult), reads=[ub, cwt], writes=[cy])
                    fw.op("pool", lambda e: e.tensor_tensor(out=v(ct1[:]), in0=usl(1), in1=cwb(1), op=ALU.mult), reads=[ub, cwt], writes=[ct1])
                    fw.op("pool", lambda e: e.tensor_tensor(out=cy[:], in0=cy[:], in1=ct1[:], op=ALU.add), reads=[cy, ct1], writes=[cy])
                    fw.op("pool", lambda e: e.tensor_tensor(out=v(ct1[:]), in0=usl(0), in1=cwb(0), op=ALU.mult), reads=[ub, cwt], writes=[ct1])
                    fw.op("pool", lambda e: e.tensor_tensor(out=cy[:], in0=cy[:], in1=ct1[:], op=ALU.add), reads=[cy, ct1], writes=[cy])
                    fw.op("pool", lambda e: e.tensor_tensor(out=bcT[:], in0=cy[:], in1=bT[:], op=ALU.mult), reads=[cy, bT], writes=[bcT])
                    if samp or t == NT_P - 1:
                        fw.op("pool", lambda e: e.tensor_copy(out=v(ct1[:]), in_=usl(2)), reads=[ub], writes=[ct1])
                        p = banks.next()
                        for c in range(4):
                            fw.op("pe", lambda e, c=c, p=p: e.transpose(out=p[:, c * 128:(c + 1) * 128], in_=ct1[:, c, :], identity=identf[:]),
                                  reads=[ct1, identf], writes=[p], inc=(c == 3))
                        fw.op("act", lambda e, p=p: e.activation(out=uf[:], in_=p[:], func=AF.Copy), reads=[p], writes=[uf])
                        if samp:
                            for b in range(16):
                                out_toks.append(fw.dma("sp", lambda e, b=b: e.dma_start(out=scs[b], in_=uf[b * 8 + 6:b * 8 + 8, :]), scs, uf))
                        elif not os.environ.get("SKIP_SCP"):
                            out_toks.append(fw.dma("sp", lambda e: e.dma_start(out=scp[:], in_=uf[126:128, :]), scp, uf))
                    if not samp:
                        fw.op("pool", lambda e: e.tensor_copy(out=ubp[:, :, 0, 0:2], in_=ubp[:, :, 0, 128:130]), reads=[ubp], writes=[ubp])
                    csb = lambda: cs[:].unsqueeze(1).to_broadcast([128, 4, 128])
                    snb = lambda: sn[:].unsqueeze(1).to_broadcast([128, 4, 128])
                    p = proj(xT, 12, 4)
                    fw.op("dve", lambda e, p=p: e.tensor_tensor(out=tq1[:], in0=v4(p), in1=csb(), op=ALU.mult), reads=[p, cs], writes=[tq1])
                    p = proj(xT, 16, 4)
                    fw.op("dve", lambda e, p=p: e.tensor_tensor(out=tq2[:], in0=v4(p), in1=snb(), op=ALU.mult), reads=[p, sn], writes=[tq2])
                    fw.op("dve", lambda e: e.tensor_tensor(out=rqT[:], in0=tq1[:], in1=tq2[:], op=ALU.add), reads=[tq1, tq2], writes=[rqT])
                    p = proj(xT, 20, 2)
                    rk = rkT[s]
                    fw.op("dve", lambda e, p=p: e.tensor_tensor(out=tk1[:], in0=p[:, 0:128], in1=cs[:], op=ALU.mult), reads=[p, cs], writes=[tk1])
                    fw.op("dve", lambda e, p=p: e.tensor_tensor(out=tk2[:], in0=p[:, 128:256], in1=sn[:], op=ALU.mult), reads=[p, sn], writes=[tk2])
                    fw.op("dve", lambda e: e.tensor_tensor(out=rk[:], in0=tk1[:], in1=tk2[:], op=ALU.add), reads=[tk1, tk2], writes=[rk])
                    last = samp or t == NT_P - 1
                    if last:
                        fw.op("pool", lambda e: e.tensor_tensor(out=rkf[:], in0=tk1[:], in1=tk2[:], op=ALU.add), reads=[tk1, tk2], writes=[rkf])
                        p = banks.next()
                        fw.op("pe", lambda e, p=p: e.transpose(out=p[:, 0:128], in_=rkf[:], identity=identf[:]), reads=[rkf, identf], writes=[p])
                        fw.op("act", lambda e, p=p: e.activation(out=kf[:], in_=p[:, 0:128], func=AF.Copy), reads=[p], writes=[kf])
                        if samp:
                            for b in range(16):
                                out_toks.append(fw.dma("sp", lambda e, b=b: e.dma_start(out=cks[b, 120:128, :], in_=kf[b * 8:(b + 1) * 8, :]), cks, kf))
                        elif not os.environ.get("SKIP_CKP"):
                            out_toks.append(fw.dma("sp", lambda e: e.dma_start(out=ckp[:], in_=kf[:]), ckp, kf))
                    if prew is not None and not samp:
                        prew.step(1)
                    p = banks.next()
                    for k in range(8):
                        fw.op("pe", lambda e, k=k, p=p: e.matmul(p[:, 0:128], lhsT=xT[:, k, :], rhs=wi[:, k, 22 * 128:23 * 128], start=(k == 0), stop=(k == 7)),
                              reads=[xT, wbuf(22)], writes=[p], inc=(k == 7))
                    for j in range(2):
                        fw.op("act", lambda e, p=p, j=j: e.activation(out=vz[s][j][:, j * 64:(j + 1) * 64], in_=p[:, j * 64:(j + 1) * 64], func=AF.Copy), reads=[p], writes=[vz[s][j]])
                    if last:
                        fw.op("act", lambda e, p=p: e.activation(out=vf[:], in_=p[:, 0:128], func=AF.Copy), reads=[p], writes=[vf])
                        if samp:
                            for b in range(16):
                                out_toks.append(fw.dma("sp", lambda e, b=b: e.dma_start(out=cvs[b, 120:128, :], in_=vf[b * 8:(b + 1) * 8, :]), cvs, vf))
                        elif not os.environ.get("SKIP_CVP"):
                            out_toks.append(fw.dma("sp", lambda e: e.dma_start(out=cvp[:], in_=vf[:]), cvp, vf))
                    if prew is not None and not samp:
                        prew.step(1)
                    blocks = []
                    if samp:
                        blocks.append((rkT[s], vz[s], 2))
                    else:
                        if ti > 0:
                            blocks.append((rkT[1 - s], vz[1 - s], 1))
                        blocks.append((rkT[s], vz[s], 0))
                    pts = {}
                    for kv in range(2):
                        lo = kv * 64
                        for bi, (rkb, vb, mi) in enumerate(blocks):
                            p = banks.next()
                            fw.op("pe", lambda e, p=p, mi=mi: e.matmul(p[:], lhsT=identb[:], rhs=masks[:, mi, :], start=True, stop=False),
                                  reads=[identb, masks], writes=[p], inc=False)
                            fw.op("pe", lambda e, p=p, rkb=rkb, lo=lo: e.matmul(p[:], lhsT=rkb[lo:lo + 64, :], rhs=rqT[lo:lo + 64, :, :].rearrange("p a b -> p (a b)"), start=False, stop=True),
                                  reads=[rkb, rqT], writes=[p])
                            pt = PT[kv * 2 + bi]
                            fw.op("act", lambda e, p=p, pt=pt: e.activation(out=pt[:], in_=p[:], func=AF.Exp, scale=0.125), reads=[p], writes=[pt])
                            pts[(kv, bi)] = pt
                    for g in range(2):
                        p = proj(xT, 23 + g * 4, 4)
                        fw.op("act", lambda e, p=p, g=g: e.activation(out=sgc[:, g * 4:(g + 1) * 4, :], in_=v4(p), func=AF.Sigmoid), reads=[p], writes=[sgc])
                    for g in range(2):
                        p = proj(xT, 31 + g * 4, 4)
                        fw.op("act", lambda e, p=p, g=g: e.activation(out=sga[:, g * 4:(g + 1) * 4, :], in_=v4(p), func=AF.Sigmoid), reads=[p], writes=[sga])
                    if samp:
                        for j in range(2):
                            fw.op("pool", lambda e, j=j: e.memset(cvz[j][:], 0.0), writes=[cvz[j]])
                            fw.dma("pool", lambda e, j=j: e.dma_start(out=cvz[j][:, :, j * 64:(j + 1) * 64], in_=cv.t.rearrange("b k c -> k b c")[:, :, j * 64:(j + 1) * 64]), cvz[j], cv)
                        pgrp = None
                        for b in range(16):
                            cb = ckb[b % 2]
                            fw.dma("pool", lambda e, b=b, cb=cb: e.dma_start(out=cb[:], in_=ck[b]), cb, ck)
                            if b % 4 == 0:
                                pgrp = banks.next()
                            pbf = pgrp[:].bitcast(BF16)
                            fw.op("pe", lambda e, b=b, cb=cb, pbf=pbf: e.transpose(out=pbf[:, (b % 4) * 128:(b % 4 + 1) * 128], in_=cb[:], identity=identb[:]),
                                  reads=[cb, identb], writes=[pgrp])
                            if b % 4 == 3:
                                g0 = b - 3
                                fw.op("dve", lambda e, g0=g0, pbf=pbf: e.tensor_copy(out=ckT[:, g0:g0 + 4, :], in_=pbf[:, 0:512].rearrange("p (a b) -> p a b", a=4)), reads=[pgrp], writes=[ckT])
                        out_toks.append(fw.dma("sp", lambda e: e.dma_start(out=cks[:, 0:120, :], in_=ck[:, 8:128, :]), cks, ck))
                        out_toks.append(fw.dma("sp", lambda e: e.dma_start(out=cvs[:, 0:120, :], in_=cv[:, 8:128, :]), cvs, cv))
                        psc = [banks.next(), banks.next()]
                        for b in range(16):
                            for kv in range(2):
                                lo = kv * 64
                                p = psc[kv]
                                off = b * 32
                                fw.op("pe", lambda e, p=p, off=off, b=b, lo=lo: e.matmul(p[:, off:off + 32].rearrange("p (a q) -> p a q", a=4), lhsT=ckT[lo:lo + 64, b, :],
                                                                                      rhs=rqT[lo:lo + 64, :, b * 8:(b + 1) * 8], start=True, stop=True),
                                      reads=[ckT, rqT], writes=[p], inc=(b == 15))
                        for kv in range(2):
                            fw.op("act", lambda e, kv=kv: e.activation(out=PC[:, kv, :, :].rearrange("p b f -> p (b f)"), in_=psc[kv][:], func=AF.Exp, scale=0.125), reads=[psc[kv]], writes=[PC])
                        fw.op("dve", lambda e: e.tensor_tensor(out=PC[:].rearrange("p k b (a q) -> p (k b a) q", q=8), in0=PC[:].rearrange("p k b (a q) -> p (k b a) q", q=8),
                                                               in1=mcache[:].unsqueeze(1).to_broadcast([128, 128, 8]), op=ALU.mult), reads=[PC, mcache], writes=[PC])
                    pn = banks.next(); pd = banks.next()
                    seq = [(kv, bi) for kv in range(2) for bi in range(len(blocks))]
                    for i, (kv, bi) in enumerate(seq):
                        vb = blocks[bi][1][kv]
                        lastmm = (i == len(seq) - 1)
                        fw.op("pe", lambda e, vb=vb, kv=kv, bi=bi, i=i, lastmm=lastmm: e.matmul(pn[:], lhsT=vb[:], rhs=pts[(kv, bi)][:], start=(i == 0), stop=lastmm),
                              reads=[vb, pts[(kv, bi)]], writes=[pn], inc=lastmm)
                    if samp:
                        pnc = banks.next()
                        for b in range(16):
                            for kv in range(2):
                                fw.op("pe", lambda e, b=b, kv=kv: e.matmul(pnc[:, b * 32:(b + 1) * 32], lhsT=cvz[kv][:, b, :], rhs=PC[:, kv, b, :], start=(kv == 0), stop=(kv == 1)),
                                      reads=[cvz[kv], PC], writes=[pnc], inc=(b == 15 and kv == 1))
                        fw.op("act", lambda e: e.activation(out=numc[:], in_=pnc[:], func=AF.Copy), reads=[pnc], writes=[numc])
                    for i, (kv, bi) in enumerate(seq):
                        lastmm = (i == len(seq) - 1)
                        fw.op("pe", lambda e, kv=kv, bi=bi, i=i, lastmm=lastmm: e.matmul(pd[:], lhsT=oz[kv][:], rhs=pts[(kv, bi)][:], start=(i == 0), stop=lastmm),
                              reads=[oz[kv], pts[(kv, bi)]], writes=[pd], inc=lastmm)
                    fw.op("dve", lambda e: e.tensor_tensor(out=den[:].rearrange("p (a n) -> p a n", a=4), in0=pd[:].rearrange("p (a n) -> p a n", a=4),
                                                           in1=es[:, 0:4].unsqueeze(2).to_broadcast([128, 4, 128]), op=ALU.add), reads=[pd, es], writes=[den])
                    if samp:
                        pdc = banks.next()
                        for kv in range(2):
                            fw.op("pe", lambda e, kv=kv: e.matmul(pdc[:], lhsT=oz[kv][:], rhs=PC[:, kv, :, :].rearrange("p b f -> p (b f)"), start=(kv == 0), stop=(kv == 1)),
                                  reads=[oz[kv], PC], writes=[pdc], inc=(kv == 1))
                        fw.op("dve", lambda e: e.tensor_tensor(out=den[:].rearrange("p (a b q) -> p a b q", a=4, q=8), in0=den[:].rearrange("p (a b q) -> p a b q", a=4, q=8),
                                                               in1=pdc[:].rearrange("p (b a q) -> p a b q", a=4, q=8), op=ALU.add), reads=[den, pdc], writes=[den])
                    fw.op("dve", lambda e: e.reciprocal(out=den[:], in_=den[:]), reads=[den], writes=[den])
                    if samp:
                        fw.op("dve", lambda e: e.tensor_tensor(out=numc[:].rearrange("p (b a q) -> p a b q", a=4, q=8), in0=pn[:].rearrange("p (a b q) -> p a b q", a=4, q=8),
                                                               in1=numc[:].rearrange("p (b a q) -> p a b q", a=4, q=8), op=ALU.add), reads=[pn, numc], writes=[numc])
                        fw.op("dve", lambda e: e.tensor_tensor(out=aoT[:].rearrange("p a (b q) -> p a b q", q=8), in0=numc[:].rearrange("p (b a q) -> p a b q", a=4, q=8),
                                                               in1=den[:].rearrange("p (a b q) -> p a b q", a=4, q=8), op=ALU.mult), reads=[numc, den], writes=[aoT])
                    else:
                        fw.op("dve", lambda e: e.tensor_tensor(out=aoT[:].rearrange("p a n -> p (a n)"), in0=pn[:], in1=den[:], op=ALU.mult), reads=[pn, den], writes=[aoT])
                    if prew is not None and not samp:
                        prew.step(1)
                    pcs = []
                    for g in range(2):
                        pc = banks.next()
                        for dc in range(4):
                            col = (g * 4 + dc) * 128
                            for k in range(4):
                                fw.op("pe", lambda e, pc=pc, dc=dc, col=col, k=k: e.matmul(pc[:, dc * 128:(dc + 1) * 128], lhsT=wco[:, k, col:col + 128], rhs=bcT[:, k, :], start=(k == 0), stop=(k == 3)),
                                      reads=[wco, bcT], writes=[pc], inc=(dc == 3 and k == 3))
                        pcs.append(pc)
                    for g in range(2):
                        pc = pcs[g]
                        pa = banks.next()
                        for dc in range(4):
                            col = (g * 4 + dc) * 128
                            for h in range(4):
                                fw.op("pe", lambda e, pa=pa, dc=dc, col=col, h=h: e.matmul(pa[:, dc * 128:(dc + 1) * 128], lhsT=wao[:, h, col:col + 128], rhs=aoT[:, h, :], start=(h == 0), stop=(h == 3)),
                                      reads=[wao, aoT], writes=[pa], inc=(dc == 3 and h == 3))
                        fw.op("dve", lambda e, pc=pc, g=g: e.tensor_tensor(out=mt1[:], in0=v4(pc), in1=sgc[:, g * 4:(g + 1) * 4, :], op=ALU.mult), reads=[pc, sgc], writes=[mt1])
                        fw.op("dve", lambda e, pa=pa, g=g: e.tensor_tensor(out=mt2[:], in0=v4(pa), in1=sga[:, g * 4:(g + 1) * 4, :], op=ALU.mult), reads=[pa, sga], writes=[mt2])
                        fw.op("dve", lambda e, g=g: e.tensor_tensor(out=mT[:, g * 4:(g + 1) * 4, :], in0=mt1[:], in1=mt2[:], op=ALU.add), reads=[mt1, mt2], writes=[mT])
                    if ti + 1 < len(tile_list):
                        emit_xT(ti + 1)
                        emit_bch(ti + 1)
                    for hf in range(2):
                        p = banks.next()
                        for k in range(8):
                            fw.op("pe", lambda e, p=p, k=k, hf=hf: e.matmul(p[:], lhsT=mT[:, k, :], rhs=wo[:, k, hf * 512:(hf + 1) * 512], start=(k == 0), stop=(k == 7)),
                                  reads=[mT, wo], writes=[p], inc=(k == 7))
                        fw.op("dve", lambda e, p=p, hf=hf: e.scalar_tensor_tensor(out=y1[:, hf * 512:(hf + 1) * 512], in0=x[:, hf * 512:(hf + 1) * 512], scalar=ALPHA, in1=p[:], op0=ALU.mult, op1=ALU.add),
                              reads=[x, p], writes=[y1])
                    h = hh[0]
                    layer_norm(fw, y1, h, g1, b1, lnp)
                    fw.dma("pool", lambda e, h=h: e.dma_start(out=HS[t], in_=h[:]), HS, h, sem_buf=h)

                if prew is not None:
                    prew.handoff(False)
                    while prew.nc_ < 128:
                        prew.step(4)
                fw.barrier()
                fw.run_block()

        if "b1" in stages:
            with contextlib.ExitStack() as ps:
                fw.cur = ps
                wpq = fw.sbuf("b_wpq", [128, 8, 2048], BF16, dma=False)
                wpqg = [fw.alias(f"b_wpqg{i}", wpq.t, dma=True) for i in range(4)]
                for gi in range(4):
                    for k in range(8):
                        fw.dma("pool", lambda e, k=k, gi=gi: e.dma_start(out=wpq[:, k, gi * 512:(gi + 1) * 512], in_=w_pq[k * 128:(k + 1) * 128, gi * 512:(gi + 1) * 512]), wpqg[gi], w_pq)
                skb = fw.sbuf("b_skb", [128, 16, 128], BF16, dma=True)
                fw.dma("pool", lambda e: e.dma_start(out=skb[:], in_=skeys.t.rearrange("j k c -> k j c")), skb, skeys)
                identb = fw.sbuf("b_identb", [128, 128], BF16, dma=True)
                fw.dma("pool", lambda e: e.dma_start(out=identb[:], in_=c_ident[:]), identb, c_ident)
                identf = fw.sbuf("b_identf", [128, 128], F32, dma=True)
                fw.dma("sp", lambda e: e.dma_start(out=identf[:], in_=c_ident[:]), identf, c_ident)
                iota16 = fw.sbuf("b_iota16", [128, 16], F32, dma=True)
                fw.dma("sp", lambda e: e.dma_start(out=iota16[:], in_=c_iota[:, 0:16]), iota16, c_iota)
                KT = fw.sbuf("b_KT", [128, 16, 128], BF16)
                banks = Rot([fw.psum(f"b_ps{i}", [128, 512], F32) for i in range(8)])
                for g in range(4):
                    p = banks.next()
                    pbf = p[:].bitcast(BF16)
                    for jj in range(4):
                        fw.op("pe", lambda e, g=g, jj=jj, pbf=pbf: e.transpose(out=pbf[:, jj * 128:(jj + 1) * 128], in_=skb[:, g * 4 + jj, :], identity=identb[:]),
                              reads=[skb, identb], writes=[p], inc=(jj == 3))
                    fw.op("dve", lambda e, g=g, pbf=pbf: e.tensor_copy(out=KT[:, g * 4:(g + 1) * 4, :], in_=pbf[:, 0:512].rearrange("p (a b) -> p a b", a=4)), reads=[p], writes=[KT])
                hfb = [fw.sbuf(f"b_hf{i}", [128, 1024], F32, dma=True) for i in range(2)]
                hb_ = [fw.sbuf(f"b_hb{i}", [128, 1024], BF16) for i in range(2)]
                hT_ = [fw.sbuf(f"b_hT{i}", [128, 8, 128], BF16, dma=True) for i in range(2)]
                qvT_ = [fw.sbuf(f"b_qvT{i}", [128, 16, 128], BF16) for i in range(2)]
                sc_ = [fw.sbuf(f"b_sc{i}", [128, 16, 128], F32) for i in range(2)]
                work = fw.sbuf("b_work", [128, 16, 128], F32)
                tv = fw.sbuf("b_tv", [128, 16, 16], F32)
                ti = fw.sbuf("b_ti", [128, 16, 16], U32)
                tif = fw.sbuf("b_tif", [128, 16, 16], BF16)
                cand = fw.sbuf("b_cand", [128, 8, 256], F32)
                cwork = fw.sbuf("b_cwork", [128, 8, 256], F32)
                best = fw.sbuf("b_best", [128, 8, 16], F32)
                pos = fw.sbuf("b_pos", [128, 8, 16], U32)
                au = fw.sbuf("b_au", [128, 8, 16], U32)
                bu = fw.sbuf("b_bu", [128, 8, 16], U32)
                af_ = fw.sbuf("b_af", [128, 8, 16], BF16)
                bf_ = fw.sbuf("b_bf", [128, 8, 16], BF16)
                eq = fw.sbuf("b_eq", [128, 8, 16, 16], BF16)
                iota16b = fw.sbuf("b_iota16b", [128, 16], BF16)
                io128i = fw.sbuf("b_io128i", [128, 128], I32)
                io256i = fw.sbuf("b_io256i", [128, 256], I32)
                fw.op("pool", lambda e: e.iota(io128i[:], pattern=[[1, 128]], base=0, channel_multiplier=0), writes=[io128i])
                fw.op("pool", lambda e: e.iota(io256i[:], pattern=[[1, 256]], base=0, channel_multiplier=0), writes=[io256i])
                msk7 = fw.sbuf("b_msk7", [128, 1], I32)
                msk8 = fw.sbuf("b_msk8", [128, 1], I32)
                fw.op("pool", lambda e: e.memset(msk7[:], -128), writes=[msk7])
                fw.op("pool", lambda e: e.memset(msk8[:], -256), writes=[msk8])
                fw.op("dve", lambda e: e.tensor_copy(out=iota16b[:], in_=iota16[:]), reads=[iota16], writes=[iota16b])
                selv = fw.sbuf("b_selv", [128, 3, 8, 16], F32)
                ex = fw.sbuf("b_ex", [128, 8, 16], F32)
                zs = fw.sbuf("b_zs", [128, 8], F32)
                selT = fw.sbuf("b_selT", [128, 3, 128], F32, dma=True)
                tvA = [fw.alias(f"tvA{j}", tv[:, j, 0:8]) for j in range(16)]
                tvB = [fw.alias(f"tvB{j}", tv[:, j, 8:16]) for j in range(16)]
                tiA = [fw.alias(f"tiA{j}", ti[:, j, 0:8]) for j in range(16)]
                tiB = [fw.alias(f"tiB{j}", ti[:, j, 8:16]) for j in range(16)]
                workj = [fw.alias(f"workj{j}", work[:, j, :]) for j in range(16)]
                bsA = [fw.alias(f"bsA{h}", best[:, h, 0:8]) for h in range(8)]
                bsB = [fw.alias(f"bsB{h}", best[:, h, 8:16]) for h in range(8)]
                psA = [fw.alias(f"psA{h}", pos[:, h, 0:8]) for h in range(8)]
                psB = [fw.alias(f"psB{h}", pos[:, h, 8:16]) for h in range(8)]
                cworkh = [fw.alias(f"cworkh{h}", cwork[:, h, :]) for h in range(8)]

                def ldh(t, s):
                    fw.dma("sp", lambda e: e.dma_start(out=hfb[s][:], in_=HS[t]), hfb[s], HS)
                ldh(tile_list[0], 0)

                vstate = [0]

                def b1_vconv(cnt):
                    if "pre" not in stages or pre_alone:
                        return
                    for _ in range(cnt):
                        if vstate[0] >= 32:
                            return
                        i = vstate[0]; vstate[0] += 1
                        fw.dma("pool", lambda e: e.dma_start(out=VB[i * 512:(i + 1) * 512, :], in_=ev[i * 512:(i + 1) * 512, :]), VB, ev)

                def b1_front(ti_):
                    t = tile_list[ti_]
                    s = ti_ % 2
                    b1_vconv(1)
                    if ti_ + 1 < len(tile_list):
                        ldh(tile_list[ti_ + 1], 1 - s)
                    hf = hfb[s]
                    hb = hb_[s]; hT = hT_[s]; qvT = qvT_[s]; sc = sc_[s]
                    fw.op("act", lambda e: e.activation(out=hb[:], in_=hf[:], func=AF.Copy), reads=[hf], writes=[hb])
                    p = banks.next(); pbf = p[:].bitcast(BF16)
                    for k in range(8):
                        fw.op("pe", lambda e, k=k, pbf=pbf: e.transpose(out=pbf[:, k * 128:(k + 1) * 128], in_=hb[:, k * 128:(k + 1) * 128], identity=identb[:]),
                              reads=[hb, identb], writes=[p], inc=(k == 7))
                    fw.op("act", lambda e, pbf=pbf: e.activation(out=hT[:], in_=pbf[:].rearrange("p (a b) -> p a b", a=8), func=AF.Copy), reads=[p], writes=[hT])
                    fw.dma("sp", lambda e: e.dma_start(out=HT[t], in_=hT[:]), HT, hT, sem_buf=hT)
                    for g in range(4):
                        p = banks.next()
                        for jj in range(4):
                            j = g * 4 + jj
                            for k in range(8):
                                fw.op("pe", lambda e, p=p, jj=jj, j=j, k=k: e.matmul(p[:, jj * 128:(jj + 1) * 128], lhsT=wpq[:, k, j * 128:(j + 1) * 128], rhs=hT[:, k, :], start=(k == 0), stop=(k == 7)),
                                      reads=[wpqg[g], hT], writes=[p], inc=(jj == 3 and k == 7))
                        fw.op("act", lambda e, p=p, g=g: e.activation(out=qvT[:, g * 4:(g + 1) * 4, :], in_=p[:].rearrange("p (a b) -> p a b", a=4), func=AF.Copy), reads=[p], writes=[qvT])
                    for g in range(4):
                        p = banks.next()
                        for jj in range(4):
                            j = g * 4 + jj
                            fw.op("pe", lambda e, p=p, jj=jj, j=j: e.matmul(p[:, jj * 128:(jj + 1) * 128], lhsT=qvT[:, j, :], rhs=KT[:, j, :], start=True, stop=True),
                                  reads=[qvT, KT], writes=[p], inc=(jj == 3))
                        fw.op("act", lambda e, p=p, g=g: e.activation(out=sc[:, g * 4:(g + 1) * 4, :], in_=p[:].rearrange("p (a b) -> p a b", a=4), func=AF.Copy), reads=[p], writes=[sc])

                def b1_back(ti_):
                    t = tile_list[ti_]
                    s = ti_ % 2
                    sc = sc_[s]
                    sci = sc[:].bitcast(I32)
                    fw.op("dve", lambda e: e.scalar_tensor_tensor(out=sci, in0=sci, scalar=msk7[:, 0:1], in1=io128i[:].unsqueeze(1).to_broadcast([128, 16, 128]),
                                                                  op0=ALU.bitwise_and, op1=ALU.bitwise_or), reads=[sc, msk7, io128i], writes=[sc])
                    for j in range(16):
                        fw.op("dve", lambda e, j=j: e.max(out=tvA[j][:], in_=sc[:, j, :]), reads=[sc], writes=[tvA[j]])
                    for j in range(16):
                        fw.op("dve", lambda e, j=j: e.match_replace(out=workj[j][:], in_to_replace=tvA[j][:], in_values=sc[:, j, :], imm_value=-1e30), reads=[sc, tvA[j]], writes=[workj[j]])
                    for j in range(16):
                        fw.op("dve", lambda e, j=j: e.max(out=tvB[j][:], in_=workj[j][:]), reads=[workj[j]], writes=[tvB[j]])
                    tv_deps = tvA + tvB
                    fw.op("dve", lambda e: e.tensor_scalar(out=ti[:].bitcast(I32), in0=tv[:].bitcast(I32), scalar1=127, scalar2=None, op0=ALU.bitwise_and), reads=tv_deps, writes=[ti])
                    ti_deps = [ti]
                    fw.op("dve", lambda e: e.tensor_copy(out=tif[:], in_=ti[:].bitcast(I32)), reads=ti_deps, writes=[tif])
                    tv4 = tv[:].rearrange("p (h t) k -> p h t k", t=2)
                    tif4 = tif[:].rearrange("p (h t) k -> p h t k", t=2)
                    fw.op("dve", lambda e: e.tensor_tensor(out=cand[:].rearrange("p h (a b) -> p h a b", a=16), in0=tv4[:, :, 0, :].unsqueeze(3).to_broadcast([128, 8, 16, 16]),
                                                           in1=tv4[:, :, 1, :].unsqueeze(2).to_broadcast([128, 8, 16, 16]), op=ALU.add), reads=tv_deps, writes=[cand])
                    cdi = cand[:].bitcast(I32)
                    fw.op("dve", lambda e: e.scalar_tensor_tensor(out=cdi, in0=cdi, scalar=msk8[:, 0:1], in1=io256i[:].unsqueeze(1).to_broadcast([128, 8, 256]),
                                                                  op0=ALU.bitwise_and, op1=ALU.bitwise_or), reads=[cand, msk8, io256i], writes=[cand])
                    for h in range(8):
                        fw.op("dve", lambda e, h=h: e.max(out=bsA[h][:], in_=cand[:, h, :]), reads=[cand], writes=[bsA[h]])
                    for h in range(8):
                        fw.op("dve", lambda e, h=h: e.match_replace(out=cworkh[h][:], in_to_replace=bsA[h][:], in_values=cand[:, h, :], imm_value=-1e30), reads=[cand, bsA[h]], writes=[cworkh[h]])
                    for h in range(8):
                        fw.op("dve", lambda e, h=h: e.max(out=bsB[h][:], in_=cworkh[h][:]), reads=[cworkh[h]], writes=[bsB[h]])
                    best_deps = bsA + bsB
                    fw.op("dve", lambda e: e.tensor_scalar(out=pos[:].bitcast(I32), in0=best[:].bitcast(I32), scalar1=255, scalar2=None, op0=ALU.bitwise_and), reads=best_deps, writes=[pos])
                    pos_deps = [pos]
                    fw.op("dve", lambda e: e.tensor_scalar(out=au[:], in0=pos[:], scalar1=4, scalar2=None, op0=ALU.logical_shift_right), reads=pos_deps, writes=[au])
                    fw.op("dve", lambda e: e.tensor_scalar(out=bu[:], in0=pos[:], scalar1=15, scalar2=None, op0=ALU.bitwise_and), reads=pos_deps, writes=[bu])
                    fw.op("dve", lambda e: e.tensor_copy(out=af_[:], in_=au[:]), reads=[au], writes=[af_])
                    fw.op("dve", lambda e: e.tensor_copy(out=bf_[:], in_=bu[:]), reads=[bu], writes=[bf_])
                    io4 = iota16b[:].unsqueeze(1).unsqueeze(1).to_broadcast([128, 8, 16, 16])
                    for which, srcf in ((0, af_), (1, bf_)):
                        fw.op("dve", lambda e, srcf=srcf: e.tensor_tensor(out=eq[:], in0=srcf[:].unsqueeze(3).to_broadcast([128, 8, 16, 16]), in1=io4, op=ALU.is_equal), reads=[srcf, iota16b], writes=[eq])
                        fw.op("dve", lambda e, which=which: e.tensor_tensor(out=eq[:], in0=eq[:], in1=tif4[:, :, which, :].unsqueeze(2).to_broadcast([128, 8, 16, 16]), op=ALU.mult), reads=[eq, tif], writes=[eq])
                        fw.op("dve", lambda e, which=which: e.tensor_reduce(out=selv[:, which, :, :], in_=eq[:], axis=AX.X, op=ALU.add), reads=[eq], writes=[selv])
                    fw.op("dve", lambda e: e.tensor_tensor(out=ex[:], in0=best[:], in1=best[:, :, 0:1].to_broadcast([128, 8, 16]), op=ALU.subtract), reads=best_deps, writes=[ex])
                    fw.op("act", lambda e: e.activation(out=ex[:], in_=ex[:], func=AF.Exp), reads=[ex], writes=[ex])
                    fw.op("dve", lambda e: e.tensor_reduce(out=zs[:], in_=ex[:], axis=AX.X, op=ALU.add), reads=[ex], writes=[zs])
                    fw.op("dve", lambda e: e.reciprocal(out=zs[:], in_=zs[:]), reads=[zs], writes=[zs])
                    fw.op("dve", lambda e: e.tensor_tensor(out=selv[:, 2, :, :], in0=ex[:], in1=zs[:].unsqueeze(2).to_broadcast([128, 8, 16]), op=ALU.mult), reads=[ex, zs], writes=[selv])
                    p = banks.next()
                    for w in range(3):
                        fw.op("pe", lambda e, p=p, w=w: e.transpose(out=p[:, w * 128:(w + 1) * 128], in_=selv[:, w, :, :].rearrange("p h k -> p (h k)"), identity=identf[:]),
                              reads=[selv, identf], writes=[p], inc=(w == 2))
                    fw.op("act", lambda e, p=p: e.activation(out=selT[:], in_=p[:, 0:384].rearrange("p (a b) -> p a b", a=3), func=AF.Copy), reads=[p], writes=[selT])
                    fw.dma("sp", lambda e: e.dma_start(out=SEL[t], in_=selT[:]), SEL, selT, sem_buf=selT)

                b1_front(0)
                for ti_ in range(len(tile_list)):
                    if ti_ + 1 < len(tile_list):
                        b1_front(ti_ + 1)
                    b1_back(ti_)
                b1_vconv(32)
                fw.barrier()
                fw.run_block()

        if "b2" in stages:
            with contextlib.ExitStack() as ps:
                fw.cur = ps
                TPT = 2
                GC = 2
                NSL = 3
                Gs = [fw.sbuf(f"c_G{i}", [128, TPT * 128, 128], BF16) for i in range(2)]
                sels = [fw.sbuf(f"c_sel{i}", [128, TPT, 3, 128], F32, dma=True) for i in range(2)]
                hTps = [fw.sbuf(f"c_hTp{i}", [128, 8, TPT * 128], BF16, dma=True) for i in range(2)]
                hfp = fw.sbuf("c_hfp", [128, TPT, 1024], F32, dma=True)
                iob = fw.sbuf("c_iob", [128, 128], BF16, dma=True)
                fw.dma("pool", lambda e: e.dma_start(out=iob[:], in_=c_iota[:]), iob, c_iota)
                g2 = fw.sbuf("c_g2", [128, 1024], F32, dma=True)
                b2 = fw.sbuf("c_b2", [128, 1024], F32, dma=True)
                fw.dma("sp", lambda e: e.dma_start(out=g2[:], in_=ln2g.t.partition_broadcast(128)), g2, ln2g)
                fw.dma("sp", lambda e: e.dma_start(out=b2[:], in_=ln2b.t.partition_broadcast(128)), b2, ln2b)
                eps = fw.sbuf("c_eps", [128, 1], F32)
                fw.op("pool", lambda e: e.memset(eps[:], LN_EPS), writes=[eps])
                rotA = Rot([fw.sbuf(f"c_A{i}", [128, 128], BF16) for i in range(8)])
                rotB = Rot([fw.sbuf(f"c_B{i}", [128, 128], BF16) for i in range(8)])
                uts = [fw.sbuf(f"c_ut{i}", [128, GC, 1024], BF16, dma=True) for i in range(NSL)]
                vbs = [fw.sbuf(f"c_vb{i}", [128, GC, 1024], BF16, dma=True) for i in range(NSL)]
                rotgl = Rot([fw.sbuf(f"c_gl{i}", [128, TPT * 128], BF16) for i in range(4)])
                rotco = Rot([fw.sbuf(f"c_co{i}", [128, TPT * 128], BF16) for i in range(4)])
                y2s = [fw.sbuf(f"c_y2{i}", [128, 1024], F32) for i in range(TPT)]
                ots = Rot([fw.sbuf(f"c_ot{i}", [128, 1024], F32, dma=True) for i in range(1)])
                lnp = dict(st6=fw.sbuf("c_st6", [128, 2, 6], F32), mv=fw.sbuf("c_mv", [128, 2], F32),
                           sd=fw.sbuf("c_sd", [128, 1], F32), rs=fw.sbuf("c_rs", [128, 1], F32),
                           nmr=fw.sbuf("c_nmr", [128, 1], F32), eps=eps)
                acc = [fw.psum(f"c_acc{i}", [128, 512], F32) for i in range(2 * TPT)]
                _pb = [fw.psum(f"c_pre{i}", [128, 512], F32) for i in range(4)]
                pre = Rot(_pb[0:3])
                gbk = Rot(_pb[3:4])
                _pt = [t_ for t_ in tile_list if t_ != NTILE - 1]
                passes = ([[NTILE - 1]] if (NTILE - 1) in tile_list else []) + [_pt[i:i + TPT] for i in range(0, len(_pt), TPT)]
                ngrp = 128 // GC

                def load_sel(pi, what=("sel", "ht")):
                    if pi >= len(passes):
                        return
                    tiles_p = passes[pi]; sel = sels[pi % 2]; hTp = hTps[pi % 2]
                    for tt, t in enumerate(tiles_p):
                        if "sel" in what:
                            fw.dma("sp", lambda e, tt=tt, t=t: e.dma_start(out=sel[:, tt, :, :], in_=SEL[t]), sel, SEL)
                        if "ht" in what:
                            fw.dma("sp", lambda e, tt=tt, t=t: e.dma_start(out=hTp[:, :, tt * 128:(tt + 1) * 128], in_=HT[t]), hTp, HT)

                class GB:
                    def __init__(self, pi):
                        self.G = Gs[pi % 2]; self.sel = sels[pi % 2]
                        self.ntok = len(passes[pi]) * 128
                        self.nd = 0
                        self.nm = 0
                        self.ab = {}
                        self.pg = None
                        self.pending = None

                    def dve(self, cnt):
                        for _ in range(cnt):
                            if self.nd >= self.ntok:
                                return
                            n = self.nd; self.nd += 1
                            tt, nn = divmod(n, 128)
                            A = rotA.next(); B = rotB.next(); sel = self.sel
                            fw.op("dve", lambda e: e.tensor_scalar(out=A[:], in0=iob[:], scalar1=sel[:, tt, 0, nn:nn + 1], scalar2=sel[:, tt, 2, nn:nn + 1], op0=ALU.is_equal, op1=ALU.mult),
                                  reads=[iob, sel], writes=[A])
                            fw.op("dve", lambda e: e.tensor_scalar(out=B[:], in0=iob[:], scalar1=sel[:, tt, 1, nn:nn + 1], scalar2=None, op0=ALU.is_equal),
                                  reads=[iob, sel], writes=[B])
                            self.ab[n] = (A, B)

                    def mm(self, cnt):
                        for _ in range(cnt):
                            if self.nm >= self.ntok:
                                return
                            n = self.nm
                            if n not in self.ab:
                                self.dve(1)
                            self.nm += 1
                            A, B = self.ab.pop(n)
                            if n % 4 == 0:
                                self.flush()
                                self.pg = gbk.next()
                            pg = self.pg; G = self.G
                            fw.op("pe", lambda e: e.matmul(pg[:, (n % 4) * 128:(n % 4 + 1) * 128], lhsT=B[:], rhs=A[:], start=True, stop=True),
                                  reads=[A, B], writes=[pg])
                            if n % 4 == 3:
                                self.flush()
                                self.pending = (pg, n)

                    def flush(self):
                        if self.pending is not None:
                            pg, n = self.pending; G = self.G
                            self.pending = None
                            fw.op("act", lambda e: e.activation(out=G[:, n - 3:n + 1, :], in_=pg[:].rearrange("p (a b) -> p a b", a=4), func=AF.Copy), reads=[pg], writes=[G])

                    def finish(self):
                        while self.nm < self.ntok:
                            self.dve(4); self.mm(4)
                        self.flush()

                def emit_ln(tt, t):
                    y2 = y2s[tt]
                    ot = ots.next()
                    layer_norm(fw, y2, ot, g2, b2, lnp, mul_eng="pool")
                    fw.dma("pool", lambda e: e.dma_start(out=yout[t], in_=ot[:]), yout, ot, sem_buf=ot)

                deferred = []
                load_sel(0)
                gb0 = GB(0)
                gb0.finish()
                for pi, tiles_p in enumerate(passes):
                    ntt = len(tiles_p)
                    ntok = ntt * 128
                    G = Gs[pi % 2]; hTp = hTps[pi % 2]
                    for tt, t in enumerate(tiles_p):
                        fw.dma("sp", lambda e, tt=tt, t=t: e.dma_start(out=hfp[:, tt, :], in_=HS[t]), hfp, HS)
                    gbn = None
                    if pi + 1 < len(passes):
                        if pi == 0:
                            load_sel(1, ("sel",))
                        load_sel(pi + 1, ("ht",))
                        gbn = GB(pi + 1)
                        gbn.dve(4)

                    gbase = pi * ngrp

                    def ldgrp(gg):
                        if gg >= len(passes) * ngrp:
                            return
                        g = gg % ngrp
                        u = uts[gg % NSL]; vb_ = vbs[gg % NSL]
                        fw.dma("sp", lambda e: e.dma_start(out=u[:], in_=UT[g * GC:(g + 1) * GC].rearrange("c p f -> p c f")), u, UT)
                        fw.dma("sp", lambda e: e.dma_start(out=vb_[:], in_=VB[g * GC * 128:(g + 1) * GC * 128, :].rearrange("(c p) f -> p c f", p=128)), vb_, VB)
                    if pi == 0:
                        for g_ in range(NSL):
                            ldgrp(g_)
                    LA = 2
                    pps = {}

                    def emit_pre(c):
                        g, cc = divmod(c, GC)
                        u = uts[(gbase + g) % NSL]
                        pp = pre.next()
                        pps[c] = pp
                        for k in range(8):
                            fw.op("pe", lambda e, k=k: e.matmul(pp[:, 0:ntok], lhsT=u[:, cc, k * 128:(k + 1) * 128], rhs=hTp[:, k, 0:ntok], start=(k == 0), stop=(k == 7)),
                                  reads=[u, hTp], writes=[pp], inc=(k == 7))

                    def emit_post(c):
                        g, cc = divmod(c, GC)
                        vb_ = vbs[(gbase + g) % NSL]
                        pp = pps.pop(c)
                        gl = rotgl.next(); co = rotco.next()
                        fw.op("act", lambda e: e.activation(out=gl[:, 0:ntok], in_=pp[:, 0:ntok], func=AF.Gelu), reads=[pp], writes=[gl])
                        fw.op("dve", lambda e: e.tensor_tensor(out=co[:, 0:ntok], in0=gl[:, 0:ntok], in1=G[:, 0:ntok, c], op=ALU.mult), reads=[gl, G], writes=[co])
                        for tt in range(ntt):
                            for h2 in range(2):
                                a = acc[tt * 2 + h2]
                                lastmm = (tt == ntt - 1 and h2 == 1)
                                fw.op("pe", lambda e, a=a, tt=tt, h2=h2: e.matmul(a[:], lhsT=co[:, tt * 128:(tt + 1) * 128], rhs=vb_[:, cc, h2 * 512:(h2 + 1) * 512], start=(c == 0), stop=(c == 127)),
                                      reads=[co, vb_], writes=[a], inc=lastmm)
                        if cc == GC - 1:
                            ldgrp(gbase + g + NSL)

                    for c in range(128 + LA):
                        if c < 128:
                            emit_pre(c)
                        if c - LA >= 0:
                            emit_post(c - LA)
                        if c in (8, 24) and deferred:
                            emit_ln(*deferred.pop(0))
                        if c == 64:
                            load_sel(pi + 2, ("sel",))
                        if gbn is not None:
                            gbn.flush()
                            gbn.dve(2)
                            gbn.mm(2)
                    if gbn is not None:
                        gbn.finish()
                    for tt, t in enumerate(tiles_p):
                        y2 = y2s[tt]
                        for h2 in range(2):
                            a = acc[tt * 2 + h2]
                            fw.op("dve", lambda e, a=a, tt=tt, h2=h2: e.scalar_tensor_tensor(out=y2[:, h2 * 512:(h2 + 1) * 512], in0=hfp[:, tt, h2 * 512:(h2 + 1) * 512], scalar=ALPHA, in1=a[:], op0=ALU.mult, op1=ALU.add),
                                  reads=[hfp, a], writes=[y2])
                    deferred = list(enumerate(tiles_p))
                for tt, t in deferred:
                    emit_ln(tt, t)
                fw.barrier()
                fw.run_block()

        fw.cur = st
        toks = [D[k].last_write for k in ("yout", "scp", "ckp", "cvp", "scs", "cks", "cvs")] + out_toks
        if debug:
            toks += [HS.last_write, SEL.last_write]
        fw.barrier()
        fw.wait_tok("sp", [t for t in toks if t is not None])
        fw.run_block()
    return nc


def _consts():
    half = 32
    inv_freq = (10000.0 ** (-2.0 * np.arange(half, dtype=np.float64) / 64.0)).astype(np.float32)
    pos = np.concatenate([np.arange(4096, dtype=np.float32), (16384 + (np.arange(128) % 8)).astype(np.float32)])
    ang = (pos[None, :] * inv_freq[:, None]).astype(np.float32)
    c = np.cos(ang.astype(np.float64)).astype(np.float32)
    s = np.sin(ang.astype(np.float64)).astype(np.float32)
    cos64 = np.concatenate([c, c], axis=0)
    sin64 = np.concatenate([-s, s], axis=0)
    ropec = np.ascontiguousarray(np.concatenate([cos64, cos64], axis=0))
    ropes = np.ascontiguousarray(np.concatenate([sin64, sin64], axis=0))
    ident = np.eye(128, dtype=np.float32)
    sidx = np.arange(128)[:, None]
    qidx = np.arange(128)[None, :]
    m_cur = np.where(sidx <= qidx, 0.0, NEG).astype(np.float32)
    m_prev = np.where(sidx > qidx, 0.0, NEG).astype(np.float32)
    m_samp = np.where((sidx <= qidx) & (sidx // 8 == qidx // 8), 0.0, NEG).astype(np.float32)
    masks = np.stack([np.tile(m, (1, 4)) for m in (m_cur, m_prev, m_samp)], axis=0)
    mcache = (np.arange(128)[:, None] > np.arange(8)[None, :]).astype(np.float32)
    iota = np.tile(np.arange(128, dtype=np.float32)[None, :], (128, 1))
    return dict(ropec=ropec, ropes=ropes, c_ident=ident, c_masks=np.ascontiguousarray(masks),
                c_mcache=np.ascontiguousarray(mcache), c_iota=iota)


def _wcols():
    cols = list(range(0, 1536))
    q0 = 1536
    for swap in (0, 32):
        for cp in range(4):
            for p in range(128):
                head = cp if p < 64 else 4 + cp
                i = p % 64
                cols.append(q0 + head * 64 + (i + swap) % 64)
    k0 = 2048
    for swap in (0, 32):
        for p in range(128):
            kv = p // 64
            i = p % 64
            cols.append(k0 + kv * 64 + (i + swap) % 64)
    cols += list(range(2176, 2304))
    cols += list(range(2304, 4352))
    assert len(cols) == WCOLS
    return np.array(cols)


def _aorows():
    rows = []
    for cp in range(4):
        for p in range(128):
            head = cp if p < 64 else 4 + cp
            rows.append(head * 64 + p % 64)
    return np.array(rows)


def make_in_maps(inp):
    C = _consts()
    f = lambda a: np.ascontiguousarray(np.asarray(a, dtype=np.float32))
    w_in_ext = f(np.asarray(inp["w_in"])[:, _wcols()])
    cw = f(np.asarray(inp["conv_w"]).reshape(3, 4, 128).transpose(2, 1, 0))
    shared = dict(w_in=w_in_ext, cw=cw, sinks=f(inp["attn_sinks"]), w_co=f(inp["w_conv_out"]), w_ao=f(np.asarray(inp["w_attn_out"])[_aorows()]),
                  w_o=f(inp["w_o"]), ln1g=f(inp["ln1_g"]), ln1b=f(inp["ln1_b"]), ln2g=f(inp["ln2_g"]), ln2b=f(inp["ln2_b"]),
                  w_pq=f(inp["w_pq"]), skeys=f(np.asarray(inp["sub_keys"]).reshape(16, 128, 128)),
                  eu=f(inp["expert_u"]), ev=f(inp["expert_v"]), **C)
    xp = np.asarray(inp["x_prompt"]); xs = np.asarray(inp["x_sample"])
    maps = []
    for i in range(NCORES):
        xin = np.concatenate([xp[i].reshape(32, 128, 1024), xs[16 * i:16 * i + 16].reshape(1, 128, 1024)], axis=0)
        m = dict(shared)
        m["xin"] = f(xin)
        m["sconv"] = f(np.asarray(inp["state_conv"])[16 * i:16 * i + 16].reshape(32, 512))
        m["ck"] = f(np.asarray(inp["cache_k"])[16 * i:16 * i + 16].reshape(16, 128, 128))
        m["cv"] = f(np.asarray(inp["cache_v"])[16 * i:16 * i + 16].reshape(16, 128, 128))
        maps.append(m)
    return maps


_NC_CACHE = {}


def kernel(**inputs):
    if "nc" not in _NC_CACHE:
        _NC_CACHE["nc"] = build_program()
    nc = _NC_CACHE["nc"]
    maps = make_in_maps(inputs)
    res = run_bass_kernel_spmd(nc, maps, core_ids=list(range(NCORES)))
    R = res.results
    y_prompt = np.stack([R[i]["yout"][:32].reshape(4096, 1024) for i in range(NCORES)], axis=0)
    y_sample = np.concatenate([R[i]["yout"][32].reshape(16, 8, 1024) for i in range(NCORES)], axis=0)
    scp = np.stack([R[i]["scp"] for i in range(NCORES)], axis=0)
    ckp = np.stack([R[i]["ckp"].reshape(128, 2, 64) for i in range(NCORES)], axis=0)
    cvp = np.stack([R[i]["cvp"].reshape(128, 2, 64) for i in range(NCORES)], axis=0)
    scs = np.concatenate([R[i]["scs"] for i in range(NCORES)], axis=0)
    cks = np.concatenate([R[i]["cks"].reshape(16, 128, 2, 64) for i in range(NCORES)], axis=0)
    cvs = np.concatenate([R[i]["cvs"].reshape(16, 128, 2, 64) for i in range(NCORES)], axis=0)
    return (y_prompt.astype(np.float32), y_sample.astype(np.float32), scp.astype(np.float32), ckp.astype(np.float32),
            cvp.astype(np.float32), scs.astype(np.float32), cks.astype(np.float32), cvs.astype(np.float32))
```
